# Optimizing a Trainium2 kernel written in Bass

```python
import math
import jax, jax.numpy as jnp
from jax import lax
import numpy as np

D_MODEL = 2048
BATCH = 4
SEQ = 4096
DEPTH = 4

D_MIX = D_MODEL
C_CONV = D_MIX // 2
HEAD_DIM = 64
N_HEADS = (D_MIX - C_CONV) // HEAD_DIM
N_KV = 2
GROUP = N_HEADS // N_KV
ATTN_W = N_HEADS * HEAD_DIM
KV_W = N_KV * HEAD_DIM
CONV_WIDTH = 31
WINDOW = 128
BLOCK = 128
LN_EPS = 1e-5
NEG_INF = -1e30
DEEPNORM_ALPHA = (2 * DEPTH) ** 0.25
DEEPNORM_BETA = (8 * DEPTH) ** -0.25

SPLIT_SIZES = (C_CONV, C_CONV, C_CONV, ATTN_W, KV_W, KV_W, ATTN_W)
D_IN = sum(SPLIT_SIZES)
SPLIT_POINTS = tuple(int(p) for p in np.cumsum(SPLIT_SIZES)[:-1])

kernel_name = "hybrid_conformer_conv_swa_sink_deepnorm"


def layer_norm(x, g, b):
    xf = x.astype(jnp.float32)
    mu = jnp.mean(xf, axis=-1, keepdims=True)
    var = jnp.mean(jnp.square(xf - mu), axis=-1, keepdims=True)
    y = (xf - mu) * lax.rsqrt(var + LN_EPS)
    return (y * g.astype(jnp.float32) + b.astype(jnp.float32)).astype(x.dtype)


def conformer_conv_branch(u_val, u_glu, conv_w, conv_b, ln_g, ln_b):
    h = u_val * jax.nn.sigmoid(u_glu)
    h = lax.conv_general_dilated(
        h, conv_w[:, None, :].astype(h.dtype), window_strides=(1,),
        padding=[(CONV_WIDTH - 1, 0)],
        dimension_numbers=("NWC", "WIO", "NWC"),
        feature_group_count=C_CONV) + conv_b
    h = layer_norm(h, ln_g, ln_b)
    return jax.nn.silu(h)


def swa_sink_attention(q, k, v, sinks):
    B, S, _ = q.shape
    nb = S // BLOCK
    q = q.reshape(B, nb, BLOCK, N_KV, GROUP, HEAD_DIM)
    k = k.reshape(B, nb, BLOCK, N_KV, HEAD_DIM)
    v = v.reshape(B, nb, BLOCK, N_KV, HEAD_DIM)
    pad = ((0, 0), (1, 0), (0, 0), (0, 0), (0, 0))
    kk = jnp.concatenate([jnp.pad(k, pad)[:, :-1], k], axis=2)
    vv = jnp.concatenate([jnp.pad(v, pad)[:, :-1], v], axis=2)
    scale = HEAD_DIM ** -0.5
    scores = jnp.einsum("bnqkgd,bnskd->bnkgqs", q, kk).astype(jnp.float32) * scale
    qi = jnp.arange(BLOCK)[:, None]
    si = jnp.arange(2 * BLOCK)[None, :]
    diff = qi + BLOCK - si
    band = (diff >= 0) & (diff < WINDOW)
    blk = jnp.arange(nb)[:, None, None]
    valid = band[None] & ((blk > 0) | (si >= BLOCK)[None])
    scores = jnp.where(valid[None, :, None, None], scores, NEG_INF)
    sink = sinks.astype(jnp.float32).reshape(1, 1, N_KV, GROUP, 1, 1)
    m = jnp.maximum(jnp.max(scores, axis=-1, keepdims=True), sink)
    p = jnp.exp(scores - m)
    denom = jnp.sum(p, axis=-1, keepdims=True) + jnp.exp(sink - m)
    probs = (p / denom).astype(vv.dtype)
    out = jnp.einsum("bnkgqs,bnskd->bnqkgd", probs, vv)
    return out.reshape(B, S, ATTN_W)


def hybrid_layer(x, w_in, b_in, conv_w, conv_b, conv_ln_g, conv_ln_b, sinks,
                 w_out, b_out, ln_g, ln_b):
    proj = jnp.einsum("bsd,de->bse", x, w_in) + b_in
    c_val, c_glu, c_gate, q, k, v, a_gate = jnp.split(proj, SPLIT_POINTS, axis=-1)
    y_conv = conformer_conv_branch(c_val, c_glu, conv_w, conv_b, conv_ln_g, conv_ln_b)
    y_conv = y_conv * jax.nn.silu(c_gate)
    y_attn = swa_sink_attention(q, k, v, sinks) * jax.nn.silu(a_gate)
    y = jnp.concatenate([y_conv, y_attn], axis=-1)
    y = jnp.einsum("bse,ed->bsd", y, w_out) + b_out
    return layer_norm(DEEPNORM_ALPHA * x + y, ln_g, ln_b)


def setup_inputs(seed: int = 0) -> dict:
    key = jax.random.key(seed)
    ks = jax.random.split(key, 12)
    f32 = jnp.float32
    x = jax.random.normal(ks[0], (BATCH, SEQ, D_MODEL), f32)
    w_in = jax.random.normal(ks[1], (DEPTH, D_MODEL, D_IN), f32) * D_MODEL ** -0.5
    b_in = jax.random.normal(ks[2], (DEPTH, D_IN), f32) * 0.02
    conv_w = jax.random.normal(ks[3], (DEPTH, CONV_WIDTH, C_CONV), f32) * CONV_WIDTH ** -0.5
    conv_b = jax.random.normal(ks[4], (DEPTH, C_CONV), f32) * 0.02
    conv_ln_g = 1.0 + 0.1 * jax.random.normal(ks[5], (DEPTH, C_CONV), f32)
    conv_ln_b = 0.02 * jax.random.normal(ks[6], (DEPTH, C_CONV), f32)
    sinks = 0.5 * jax.random.normal(ks[7], (DEPTH, N_HEADS), f32)
    w_out = (jax.random.normal(ks[8], (DEPTH, D_MIX, D_MODEL), f32)
             * D_MIX ** -0.5 * DEEPNORM_BETA)
    b_out = jax.random.normal(ks[9], (DEPTH, D_MODEL), f32) * 0.02
    ln_g = 1.0 + 0.1 * jax.random.normal(ks[10], (DEPTH, D_MODEL), f32)
    ln_b = 0.02 * jax.random.normal(ks[11], (DEPTH, D_MODEL), f32)
    return {"x": x, "w_in": w_in, "b_in": b_in, "conv_w": conv_w, "conv_b": conv_b,
            "conv_ln_g": conv_ln_g, "conv_ln_b": conv_ln_b, "sinks": sinks,
            "w_out": w_out, "b_out": b_out, "ln_g": ln_g, "ln_b": ln_b}


def reference(x, w_in, b_in, conv_w, conv_b, conv_ln_g, conv_ln_b, sinks,
              w_out, b_out, ln_g, ln_b):
    h = x
    for l in range(DEPTH):
        h = hybrid_layer(h, w_in[l], b_in[l], conv_w[l], conv_b[l], conv_ln_g[l],
                         conv_ln_b[l], sinks[l], w_out[l], b_out[l], ln_g[l], ln_b[l])
    return h
```

```python
from contextlib import ExitStack

import numpy as np
import concourse.bass as bass
import concourse.mybir as mybir
from concourse.bass_utils import run_bass_kernel_spmd

F32 = mybir.dt.float32
BF16 = mybir.dt.bfloat16
AF = mybir.ActivationFunctionType
ALU = mybir.AluOpType

D = 2048
CC = 1024
TT = 512
NBLK = 4
KC = 16
CW = 31
PRE = 30
HB_W = 544
ALPHA = float(8 ** 0.25)
EPS = 1e-5
NG_IN = 11
NG = 15
NPP = 44 + 8 * CW + 24
PP_CW = 44
PP_CB = 44 + 8 * CW
PP_LG = PP_CB + 8
PP_LB = PP_LG + 8
C_MA, C_MB, C_MB0, C_ID, C_ONES, C_OHBO, C_OHSK = 0, 128, 256, 384, 512, 640, 1152
NCST = 1664
EPOCH = 8000
LOOKBACK = 6
NWR = 2
NTF = 8
NTB = 4
NPT = 4
NXBF = 2

def _in_chunks():
    ch = []
    for g in range(4):
        for c in (2 * g, 2 * g + 1):
            ch.append(("val", c))
            ch.append(("glu", c))
    ch += [("kd", 0), ("kd", 1), ("v", 0), ("pad", 0)]
    ch += [("q", i) for i in range(8)]
    ch += [("gA", i) for i in range(8)]
    ch += [("gB", i) for i in range(8)]
    return ch


IN_CHUNKS = _in_chunks()


def _chunk_cols(kind, i):
    if kind == "val":
        return np.arange(i * 128, (i + 1) * 128)
    if kind == "glu":
        return 1024 + np.arange(i * 128, (i + 1) * 128)
    if kind == "gA":
        return 2048 + np.arange(i * 128, (i + 1) * 128)
    if kind == "q":
        return 3072 + np.arange(i * 128, (i + 1) * 128)
    if kind == "kd":
        base = 4096 + i * 64 + np.arange(64)
        return np.concatenate([base, base])
    if kind == "v":
        return 4224 + np.arange(128)
    if kind == "gB":
        return 4352 + np.arange(i * 128, (i + 1) * 128)
    return np.zeros(128, dtype=np.int64)


class Op:
    __slots__ = ("eng", "fn", "deps", "idx", "dma", "tick", "needs_inc", "dval")


class Sched:
    ENGS = ("pe", "act", "dve", "pool", "sp")

    def __init__(self):
        self.ops = {e: [] for e in self.ENGS}
        self.lw = {}
        self.rd = {}
        self.dma_cnt = {}
        self.rr = {}

    def alloc(self, name, n):
        i = self.rr.get(name, 0)
        self.rr[name] = (i + 1) % n
        return i

    def add(self, eng, fn, reads=(), writes=(), dma=None):
        op = Op()
        op.eng = eng
        op.fn = fn
        op.dma = dma
        op.tick = 0
        op.needs_inc = False
        op.dval = 0
        deps = []
        for k in reads:
            w = self.lw.get(k)
            if w is not None:
                deps.append(w)
        for k in writes:
            w = self.lw.get(k)
            if w is not None:
                deps.append(w)
            deps.extend(self.rd.get(k, ()))
        seen = set()
        ud = []
        for d in deps:
            if d is op or id(d) in seen:
                continue
            seen.add(id(d))
            ud.append(d)
        op.deps = ud
        for k in reads:
            self.rd.setdefault(k, []).append(op)
        for k in writes:
            self.lw[k] = op
            self.rd[k] = []
        if dma is not None:
            c = self.dma_cnt.get(dma, 0) + 16
            self.dma_cnt[dma] = c
            op.dval = c
        op.idx = len(self.ops[eng])
        self.ops[eng].append(op)
        return op

    def finalize(self):
        for e in self.ENGS:
            for op in self.ops[e]:
                for d in op.deps:
                    if d.dma is None:
                        d.needs_inc = True
        self.nticks = {}
        for e in self.ENGS:
            t = 0
            for op in self.ops[e]:
                if op.dma is None and op.needs_inc:
                    t += 1
                    op.tick = t
            self.nticks[e] = t

    def emit(self, eng, e, eng_sems, dma_sems):
        seen = {}
        for op in self.ops[eng]:
            for d in op.deps:
                if d.dma is not None:
                    key = ("dma", d.dma)
                    if seen.get(key, 0) >= d.dval:
                        continue
                    seen[key] = d.dval
                    e.wait_ge(dma_sems[d.dma], d.dval)
                else:
                    if d.eng == eng and (eng == "pe" or d.idx < op.idx - LOOKBACK):
                        continue
                    if seen.get(d.eng, 0) >= d.tick:
                        continue
                    seen[d.eng] = d.tick
                    s, v = divmod(d.tick - 1, EPOCH)
                    e.wait_ge(eng_sems[d.eng][s], v + 1)
            ins = op.fn(e)
            if ins is None:
                continue
            if op.dma is not None:
                ins.then_inc(dma_sems[op.dma], 16)
            elif op.needs_inc:
                s, v = divmod(op.tick - 1, EPOCH)
                ins.then_inc(eng_sems[eng][s], 1)


def build_nc(L, NT, main_from=1, pool_conv=(), mask_eng="pool", gate_eng="pool"):
    nc = bass.Bass("TRN2", target_bir_lowering=False)
    NOUT = NT - main_from
    x_in = nc.dram_tensor("x_in", [NT * TT, D], F32, kind="ExternalInput").ap()
    wall = nc.dram_tensor("wall", [L * NG, 128, 8192], F32, kind="ExternalInput").ap()
    pp_in = nc.dram_tensor("pp_in", [128, L * NPP], F32, kind="ExternalInput").ap()
    cst_in = nc.dram_tensor("cst_in", [128, NCST], F32, kind="ExternalInput").ap()
    sk_in = nc.dram_tensor("sk_in", [4, L * 4], F32, kind="ExternalInput").ap()
    bo_in = nc.dram_tensor("bo_in", [4, L * 512], F32, kind="ExternalInput").ap()
    bv_in = nc.dram_tensor("bv_in", [1, L * 128], F32, kind="ExternalInput").ap()
    lng_in = nc.dram_tensor("lng_in", [L, D], F32, kind="ExternalInput").ap()
    lnb_in = nc.dram_tensor("lnb_in", [L, D], F32, kind="ExternalInput").ap()
    m_in = nc.dram_tensor("m_in", [128, 1], F32, kind="ExternalInput").ap()
    y_out = nc.dram_tensor("y", [NOUT * TT, D], F32, kind="ExternalOutput").ap()
    wbf = nc.dram_tensor("wbf", [L * NG, 128, 8192], BF16).ap()

    S = Sched()
    es = ExitStack()

    def sb(name, shape, dt):
        return es.enter_context(nc.sbuf_tensor(name, shape, dt))

    x_tok = sb("x_tok", [128, NBLK, D], F32)
    xT = sb("xT", [128, KC, TT], BF16)
    hb = sb("hb", [128, 8, HB_W], BF16)
    gA = sb("gA", [128, 8, TT], BF16)
    gB = sb("gB", [128, 8, TT], BF16)
    qT = sb("qT", [128, 8, TT], BF16)
    kT = sb("kT", [128, 2, 640], BF16)
    vaug = sb("vaug", [128, 5, 4, 128], BF16)
    U = sb("U", [128, 8, TT], F32)
    pT = [sb(f"pT{i}", [128, 2, TT], BF16) for i in range(NPT)]
    xbf = [sb(f"xbf{i}", [128, D], BF16) for i in range(NXBF)]
    tf = [sb(f"tf{i}", [128, TT], F32) for i in range(NTF)]
    tb = [sb(f"tb{i}", [128, TT], BF16) for i in range(NTB)]
    wr = [sb(f"wr{i}", [128, KC, TT], BF16) for i in range(NWR)]
    pp = sb("pp", [128, L * NPP], F32)
    cst = sb("cst", [128, NCST], BF16)
    skf = sb("skf", [4, L * 4], F32)
    ske = sb("ske", [4, L * 4], F32)
    sk4 = sb("sk4", [4, L * 4, 128], BF16)
    bo4 = sb("bo4", [4, L * 512], BF16)
    bvbc = sb("bvbc", [128, L * 128], F32)
    mcol = sb("mcol", [128, 1], F32)
    hst = sb("hst", [128, L, 8, PRE], BF16)
    kst = sb("kst", [128, L, 2, 128], BF16)
    vst = sb("vst", [128, L, 4, 128], BF16)
    sm = sb("sm", [128, NBLK, 32], F32)
    ps = [es.enter_context(nc.psum_tensor(f"ps{i}", [128, 512], F32)) for i in range(8)]

    def ppc(l, off):
        return pp[:, l * NPP + off:l * NPP + off + 1]

    def cs(off, n=128, rows=None):
        if rows is None:
            return cst[:, off:off + n]
        return cst[0:rows, off:off + n]

    def psb():
        return S.alloc("ps", 8)

    S.add("sp", lambda e: e.dma_start(out=pp[:, :], in_=pp_in), writes=[("pp",)], dma=("su", 0))
    S.add("sp", lambda e: e.dma_start(out=skf[:, :], in_=sk_in), writes=[("skf",)], dma=("su", 1))
    S.add("sp", lambda e: e.dma_start(out=mcol[:, :], in_=m_in), writes=[("mcol",)], dma=("su", 2))
    S.add("sp", lambda e: e.dma_start(out=bvbc[:, :], in_=bv_in.broadcast_to([128, L * 128])),
          writes=[("bvbc",)], dma=("su", 3))
    S.add("pool", lambda e: e.dma_start(out=cst[:, :], in_=cst_in), writes=[("cst",)], dma=("su", 4))
    S.add("pool", lambda e: e.dma_start(out=bo4[:, :], in_=bo_in), writes=[("bo4",)], dma=("su", 5))
    cast_ops = {}
    for l in range(L):
        for g in range(NG):
            i = l * NG + g
            grp = ("wc", i) if l == 0 else ("wcl", l)
            cast_ops[(l, g)] = S.add(
                "pool", (lambda e, i=i: e.dma_start(out=wbf[i], in_=wall[i])),
                writes=[("wbf", l, g)], dma=grp)
        if l > 0:
            for g in range(NG):
                cast_ops[(l, g)].dval = 16 * NG
    S.add("pool", lambda e: e.memset(vaug[:, :, :, :], 1.0), writes=[("v", b) for b in range(5)])
    S.add("pool", lambda e: e.memset(vst[:, :, :, :], 1.0), writes=[("vst", l) for l in range(L)])
    S.add("pool", lambda e: e.memset(kst[:, :, :, :], 0.0), writes=[("kst", l) for l in range(L)])
    S.add("pool", lambda e: e.memset(hst[:, :, :, :], 0.0), writes=[("hst", l) for l in range(L)])
    S.add("pool", lambda e: e.memset(hb[:, :, :], 0.0), writes=[("h", c) for c in range(8)])
    S.add("act", lambda e: e.activation(out=ske[:, :], in_=skf[:, :], func=AF.Exp),
          reads=[("skf",)], writes=[("ske",)])
    S.add("dve", lambda e: e.tensor_copy(
        out=sk4[:, :, :], in_=ske[:, :].unsqueeze(2).broadcast_to([4, L * 4, 128])),
          reads=[("ske",)], writes=[("sk4",)])

    def load_w(l, g):
        slot = S.alloc("wr", NWR)
        i = l * NG + g
        S.add("sp", (lambda e, i=i, slot=slot: e.dma_start(
            out=wr[slot][:, :, :], in_=wbf[i].rearrange("p (k e) -> p k e", k=KC))),
              reads=[("wbf", l, g)], writes=[("w", slot)], dma=("wr", slot))
        return slot

    def prep(b):
        xi = S.alloc("xbf", NXBF)
        S.add("pool", (lambda e, b=b, xi=xi: e.tensor_copy(out=xbf[xi][:, :], in_=x_tok[:, b, :])),
              reads=[("x", b)], writes=[("xbf", xi)])
        for hlf in range(2):
            bank = psb()
            pv = ps[bank].bitcast(BF16)

            def tfn(e, hlf=hlf, xi=xi, pv=pv):
                ins = None
                for j in range(8):
                    kc = hlf * 8 + j
                    ins = e.transpose(out=pv[:, j * 128:(j + 1) * 128],
                                      in_=xbf[xi][:, kc * 128:(kc + 1) * 128],
                                      identity=cs(C_ID))
                return ins
            S.add("pe", tfn, reads=[("xbf", xi), ("cst",)], writes=[("ps", bank)])
            eng = "act" if hlf == 0 else "dve"

            def efn(e, hlf=hlf, b=b, pv=pv, eng=eng):
                o = xT[:, hlf * 8:(hlf + 1) * 8, b * 128:(b + 1) * 128]
                i_ = pv[:, 0:1024].rearrange("p (a q) -> p a q", a=8)
                if eng == "act":
                    return e.activation(out=o, in_=i_, func=AF.Copy)
                return e.tensor_copy(out=o, in_=i_)
            S.add(eng, efn, reads=[("ps", bank)], writes=[("xT", b, hlf)])

    XT_KEYS = [("xT", b, h) for b in range(NBLK) for h in range(2)]

    def conv_chunk(l, c, eng):
        for k in range(CW):
            def fn(e, k=k, c=c, l=l):
                src = hb[:, c, k:k + TT]
                wk = ppc(l, PP_CW + c * CW + k)
                if k == 0:
                    return e.tensor_scalar(out=U[:, c, :], in0=src, scalar1=wk,
                                           scalar2=ppc(l, PP_CB + c), op0=ALU.mult, op1=ALU.add)
                if eng == "dve":
                    return e.scalar_tensor_tensor(out=U[:, c, :], in0=src, scalar=wk,
                                                  in1=U[:, c, :], op0=ALU.mult, op1=ALU.add)
                return None
            if eng == "dve" or k == 0:
                S.add(eng, fn, reads=[("h", c), ("pp",), ("U", c)] if k else [("h", c), ("pp",)],
                      writes=[("U", c)])
            else:
                ti = S.alloc("tf", NTF)
                S.add("pool", (lambda e, k=k, c=c, l=l, ti=ti: e.tensor_scalar(
                    out=tf[ti][:, :], in0=hb[:, c, k:k + TT], scalar1=ppc(l, PP_CW + c * CW + k),
                    scalar2=None, op0=ALU.mult)),
                      reads=[("h", c), ("pp",)], writes=[("tf", ti)])
                S.add("pool", (lambda e, c=c, ti=ti: e.tensor_tensor(
                    out=U[:, c, :], in0=U[:, c, :], in1=tf[ti][:, :], op=ALU.add)),
                      reads=[("tf", ti), ("U", c)], writes=[("U", c)])

    def tile_layer(t, l, full):
        S.add("pool", lambda e: e.tensor_copy(out=hb[:, :, 0:PRE], in_=hst[:, l, :, :]),
              reads=[("hst", l)], writes=[("h", c) for c in range(8)])
        S.add("pool", lambda e: e.tensor_copy(out=kT[:, :, 0:128], in_=kst[:, l, :, :]),
              reads=[("kst", l)], writes=[("k", 0), ("k", 1)])
        S.add("pool", lambda e: e.tensor_copy(out=vaug[:, 0, :, :], in_=vst[:, l, :, :]),
              reads=[("vst", l)], writes=[("v", 0)])

        def proj_fn(bank, slot, j):
            def fn(e):
                ins = None
                for kc in range(KC):
                    ins = e.matmul(ps[bank][:, :], lhsT=wr[slot][:, kc, j * 128:(j + 1) * 128],
                                   rhs=xT[:, kc, :], start=(kc == 0), stop=(kc == KC - 1))
                return ins
            return fn

        ngroups = NG_IN if full else 5
        for g in range(ngroups):
            slot = load_w(l, g)
            for j in range(4):
                ci = g * 4 + j
                kind, idx = IN_CHUNKS[ci]
                bcol = ppc(l, ci)
                if kind == "pad":
                    continue
                if kind == "v":
                    bank = psb()

                    def vfn(e, bank=bank, slot=slot, j=j):
                        ins = None
                        for b in range(NBLK):
                            for kc in range(KC):
                                ins = e.matmul(ps[bank][:, b * 128:(b + 1) * 128],
                                               lhsT=xT[:, kc, b * 128:(b + 1) * 128],
                                               rhs=wr[slot][:, kc, j * 128:(j + 1) * 128],
                                               start=(kc == 0), stop=(kc == KC - 1))
                        return ins
                    S.add("pe", vfn, reads=[("w", slot)] + XT_KEYS, writes=[("ps", bank)])
                    for b in range(NBLK):
                        for par in range(2):
                            def vev(e, b=b, par=par, bank=bank):
                                o = vaug[:, 1 + b, par:4:2, par * 64:par * 64 + 64]
                                i0 = ps[bank][:, b * 128:(b + 1) * 128].rearrange("p (u d) -> p u d", u=2)
                                i1 = bvbc[:, l * 128:(l + 1) * 128].rearrange("p (u d) -> p u d", u=2)
                                return e.tensor_tensor(out=o, in0=i0, in1=i1, op=ALU.add)
                            S.add("dve", vev, reads=[("ps", bank), ("bvbc",)], writes=[("v", 1 + b)])
                    continue
                if kind == "glu":
                    continue
                bank = psb()
                S.add("pe", proj_fn(bank, slot, j), reads=[("w", slot)] + XT_KEYS,
                      writes=[("ps", bank)])
                if kind == "val":
                    bank_g = psb()
                    S.add("pe", proj_fn(bank_g, slot, j + 1), reads=[("w", slot)] + XT_KEYS,
                          writes=[("ps", bank_g)])
                    ti = S.alloc("tf", NTF)
                    bg = ppc(l, ci + 1)
                    S.add("act", (lambda e, bank_g=bank_g, ti=ti, bg=bg: e.activation(
                        out=tf[ti][:, :], in_=ps[bank_g][:, :], func=AF.Sigmoid, bias=bg)),
                          reads=[("ps", bank_g), ("pp",)], writes=[("tf", ti)])
                    S.add("dve", (lambda e, bank=bank, ti=ti, idx=idx, bcol=bcol: e.scalar_tensor_tensor(
                        out=hb[:, idx, PRE:PRE + TT], in0=ps[bank][:, :], scalar=bcol,
                        in1=tf[ti][:, :], op0=ALU.add, op1=ALU.mult)),
                          reads=[("ps", bank), ("tf", ti), ("pp",)], writes=[("h", idx)])
                    if full:
                        conv_chunk(l, idx, "pool" if idx in pool_conv else "dve")
                    continue
                if kind == "kd":
                    S.add("act", (lambda e, bank=bank, idx=idx, bcol=bcol: e.activation(
                        out=kT[:, idx, 128:640], in_=ps[bank][:, :], func=AF.Identity, bias=bcol)),
                          reads=[("ps", bank), ("pp",)], writes=[("k", idx)])
                elif kind == "q":
                    S.add("act", (lambda e, bank=bank, idx=idx, bcol=bcol: e.activation(
                        out=qT[:, idx, :], in_=ps[bank][:, :], func=AF.Identity, bias=bcol)),
                          reads=[("ps", bank), ("pp",)], writes=[("q", idx)])
                elif kind in ("gA", "gB"):
                    dst = gA if kind == "gA" else gB
                    S.add("act", (lambda e, bank=bank, idx=idx, bcol=bcol, dst=dst: e.activation(
                        out=dst[:, idx, :], in_=ps[bank][:, :], func=AF.Silu, bias=bcol)),
                          reads=[("ps", bank), ("pp",)], writes=[(kind, idx)])

        if t == 0:
            S.add("pool", lambda e: e.tensor_scalar(
                out=hst[:, l, :, :], in0=hb[:, :, TT:TT + PRE], scalar1=mcol[:, 0:1], scalar2=None,
                op0=ALU.mult),
                  reads=[("h", c) for c in range(8)] + [("mcol",)], writes=[("hst", l)])
        else:
            S.add("pool", lambda e: e.tensor_copy(out=hst[:, l, :, :], in_=hb[:, :, TT:TT + PRE]),
                  reads=[("h", c) for c in range(8)], writes=[("hst", l)])
        S.add("pool", lambda e: e.tensor_copy(out=kst[:, l, :, :], in_=kT[:, :, TT:TT + 128]),
              reads=[("k", 0), ("k", 1)], writes=[("kst", l)])
        S.add("pool", lambda e: e.tensor_copy(out=vst[:, l, :, :], in_=vaug[:, 4, :, :]),
              reads=[("v", 4)], writes=[("vst", l)])
        if not full:
            return

        for qb in range(NBLK):
            for u in range(2):
                for par in range(2):
                    pr = slice(par * 64, par * 64 + 64)
                    orow = slice(par * 64, par * 64 + 64)
                    drow = slice((1 - par) * 64, (1 - par) * 64 + 64)
                    r = par * 2 + u
                    var = u * 2 + par
                    pslot = S.alloc("pT", NPT)
                    banks = []
                    for kb in range(2):
                        bank = psb()
                        banks.append(bank)
                        k0 = (qb + kb) * 128

                        def sfn(e, bank=bank, k0=k0, u=u, pr=pr, qb=qb):
                            return e.matmul(
                                ps[bank][:, :].rearrange("p (a q) -> p a q", a=4),
                                lhsT=kT[pr, u, k0:k0 + 128],
                                rhs=qT[pr, 4 * u:4 * u + 4, qb * 128:(qb + 1) * 128],
                                start=True, stop=True)
                        S.add("pe", sfn, reads=[("k", u)] + [("q", 4 * u + a) for a in range(4)],
                              writes=[("ps", bank)])
                    for kb in range(2):
                        S.add("act", (lambda e, bank=banks[kb], pslot=pslot, kb=kb: e.activation(
                            out=pT[pslot][:, kb, :], in_=ps[bank][:, :], func=AF.Exp, scale=0.125)),
                              reads=[("ps", banks[kb])], writes=[("pT", pslot, kb)])
                        if kb == 0:
                            moff = C_MB0 if (t == main_from and qb == 0 and main_from > 0) else C_MB
                        else:
                            moff = C_MA

                        def mfn(e, pslot=pslot, kb=kb, moff=moff):
                            v_ = pT[pslot][:, kb, :].rearrange("p (a q) -> p a q", a=4)
                            return e.tensor_tensor(
                                out=v_, in0=v_,
                                in1=cs(moff).unsqueeze(1).broadcast_to([128, 4, 128]),
                                op=ALU.mult)
                        S.add(mask_eng, mfn, reads=[("pT", pslot, kb), ("cst",)],
                              writes=[("pT", pslot, kb)])
                    bank = psb()

                    def pvfn(e, bank=bank, pslot=pslot, qb=qb, var=var, r=r):
                        e.matmul(ps[bank][:, :], lhsT=vaug[:, qb, var, :], rhs=pT[pslot][:, 0, :],
                                 start=True, stop=False)
                        e.matmul(ps[bank][:, :], lhsT=vaug[:, qb + 1, var, :], rhs=pT[pslot][:, 1, :],
                                 start=False, stop=False)
                        return e.matmul(ps[bank][:, :], lhsT=cs(C_OHSK + r * 128, 128, rows=4),
                                        rhs=sk4[:, l * 4:(l + 1) * 4, :], start=False, stop=True)
                    S.add("pe", pvfn, reads=[("v", qb), ("v", qb + 1), ("pT", pslot, 0),
                                             ("pT", pslot, 1), ("cst",), ("sk4",)],
                          writes=[("ps", bank)])
                    t1 = S.alloc("tf", NTF)
                    S.add("dve", (lambda e, bank=bank, t1=t1, orow=orow, drow=drow: e.reciprocal(
                        out=tf[t1][orow, :], in_=ps[bank][drow, :])),
                          reads=[("ps", bank)], writes=[("tf", t1)])
                    t2 = S.alloc("tf", NTF)
                    S.add("dve", (lambda e, bank=bank, t1=t1, t2=t2, orow=orow: e.tensor_tensor(
                        out=tf[t2][orow, :], in0=ps[bank][orow, :], in1=tf[t1][orow, :], op=ALU.mult)),
                          reads=[("ps", bank), ("tf", t1)], writes=[("tf", t2)])

                    def gfn(e, t2=t2, orow=orow, u=u, qb=qb):
                        o = gB[orow, 4 * u:4 * u + 4, qb * 128:(qb + 1) * 128]
                        return e.tensor_tensor(
                            out=o, in0=o, in1=tf[t2][orow, :].rearrange("p (a q) -> p a q", a=4),
                            op=ALU.mult)
                    S.add(gate_eng, gfn, reads=[("tf", t2)] + [("gB", 4 * u + a) for a in range(4)],
                          writes=[("gB", 4 * u + a) for a in range(4)])

        b1 = psb()
        b2 = psb()
        for c in range(8):
            i1 = S.alloc("tb", NTB)
            S.add("act", (lambda e, c=c, i1=i1: e.activation(out=tb[i1][:, :], in_=U[:, c, :], func=AF.Copy)),
                  reads=[("U", c)], writes=[("tb", i1)])
            i2 = S.alloc("tb", NTB)
            S.add("act", (lambda e, c=c, i2=i2: e.activation(out=tb[i2][:, :], in_=U[:, c, :], func=AF.Square)),
                  reads=[("U", c)], writes=[("tb", i2)])

            def stf(e, c=c, i1=i1, i2=i2):
                e.matmul(ps[b1][:, :], lhsT=cs(C_ONES), rhs=tb[i1][:, :], start=(c == 0), stop=(c == 7))
                return e.matmul(ps[b2][:, :], lhsT=cs(C_ONES), rhs=tb[i2][:, :], start=(c == 0), stop=(c == 7))
            S.add("pe", stf, reads=[("tb", i1), ("tb", i2), ("cst",)], writes=[("ps", b1), ("ps", b2)])
        imu = S.alloc("tf", NTF)
        S.add("dve", lambda e: e.tensor_scalar(out=tf[imu][:, :], in0=ps[b1][:, :], scalar1=1.0 / CC,
                                               scalar2=None, op0=ALU.mult),
              reads=[("ps", b1)], writes=[("tf", imu)])
        imq = S.alloc("tf", NTF)
        S.add("dve", lambda e: e.tensor_tensor(out=tf[imq][:, :], in0=tf[imu][:, :], in1=tf[imu][:, :],
                                               op=ALU.mult),
              reads=[("tf", imu)], writes=[("tf", imq)])
        ivar = S.alloc("tf", NTF)
        S.add("dve", lambda e: e.scalar_tensor_tensor(out=tf[ivar][:, :], in0=ps[b2][:, :], scalar=1.0 / CC,
                                                      in1=tf[imq][:, :], op0=ALU.mult, op1=ALU.subtract),
              reads=[("ps", b2), ("tf", imq)], writes=[("tf", ivar)])
        S.add("act", lambda e: e.activation(out=tf[imq][:, :], in_=tf[ivar][:, :], func=AF.Sqrt, bias=EPS),
              reads=[("tf", ivar)], writes=[("tf", imq)])
        S.add("dve", lambda e: e.reciprocal(out=tf[ivar][:, :], in_=tf[imq][:, :]),
              reads=[("tf", imq)], writes=[("tf", ivar)])
        irs = ivar
        for c in range(8):
            S.add("dve", (lambda e, c=c: e.tensor_tensor(out=U[:, c, :], in0=U[:, c, :], in1=tf[imu][:, :],
                                                         op=ALU.subtract)),
                  reads=[("U", c), ("tf", imu)], writes=[("U", c)])
            S.add("dve", (lambda e, c=c: e.tensor_tensor(out=U[:, c, :], in0=U[:, c, :], in1=tf[irs][:, :],
                                                         op=ALU.mult)),
                  reads=[("U", c), ("tf", irs)], writes=[("U", c)])
            i1 = S.alloc("tb", NTB)
            S.add("act", (lambda e, c=c, i1=i1: e.activation(
                out=tb[i1][:, :], in_=U[:, c, :], func=AF.Silu, scale=ppc(l, PP_LG + c),
                bias=ppc(l, PP_LB + c))),
                  reads=[("U", c), ("pp",)], writes=[("tb", i1)])
            S.add(gate_eng, (lambda e, c=c, i1=i1: e.tensor_tensor(
                out=gA[:, c, :], in0=gA[:, c, :], in1=tb[i1][:, :], op=ALU.mult)),
                  reads=[("tb", i1), ("gA", c)], writes=[("gA", c)])
        S.add("sp", lambda e: e.dma_start(out=U[:, 0:4, :].rearrange("p a q -> p (a q)"),
                                          in_=lng_in[l:l + 1, :].broadcast_to([128, D])),
              writes=[("U", c) for c in range(4)], dma=("lg", 0))
        S.add("sp", lambda e: e.dma_start(out=U[:, 4:8, :].rearrange("p a q -> p (a q)"),
                                          in_=lnb_in[l:l + 1, :].broadcast_to([128, D])),
              writes=[("U", c) for c in range(4, 8)], dma=("lg", 1))

        for dg in range(4):
            slot = load_w(l, NG_IN + dg)
            for b in range(NBLK):
                bank = psb()

                def ofn(e, bank=bank, slot=slot, b=b, dg=dg):
                    for ec in range(16):
                        ysrc = gA if ec < 8 else gB
                        e.matmul(ps[bank][:, :], lhsT=ysrc[:, ec % 8, b * 128:(b + 1) * 128],
                                 rhs=wr[slot][:, ec, :], start=(ec == 0), stop=False)
                    return e.matmul(ps[bank][:, :], lhsT=cs(C_OHBO + dg * 128, 128, rows=4),
                                    rhs=bo4[:, l * 512:(l + 1) * 512], start=False, stop=True)
                S.add("pe", ofn, reads=[("w", slot), ("cst",), ("bo4",)] +
                      [("gA", c) for c in range(8)] + [("gB", c) for c in range(8)],
                      writes=[("ps", bank)])
                S.add("dve", (lambda e, bank=bank, b=b, dg=dg: e.scalar_tensor_tensor(
                    out=x_tok[:, b, dg * 512:(dg + 1) * 512], in0=x_tok[:, b, dg * 512:(dg + 1) * 512],
                    scalar=ALPHA, in1=ps[bank][:, :], op0=ALU.mult, op1=ALU.add)),
                      reads=[("ps", bank), ("x", b)], writes=[("x", b)])
        last = (l == L - 1)
        for b in range(NBLK):
            for dg in range(4):
                S.add("dve", (lambda e, b=b, dg=dg: e.bn_stats(
                    out=sm[:, b, dg * 6:(dg + 1) * 6], in_=x_tok[:, b, dg * 512:(dg + 1) * 512])),
                      reads=[("x", b)], writes=[("sm", b, dg)])
            S.add("dve", (lambda e, b=b: e.bn_aggr(out=sm[:, b, 24:26], in_=sm[:, b, 0:24])),
                  reads=[("sm", b, dg) for dg in range(4)], writes=[("sm", b, 4)])
            S.add("act", (lambda e, b=b: e.activation(out=sm[:, b, 26:27], in_=sm[:, b, 25:26],
                                                      func=AF.Sqrt, bias=EPS)),
                  reads=[("sm", b, 4)], writes=[("sm", b, 5)])
            S.add("dve", (lambda e, b=b: e.reciprocal(out=sm[:, b, 27:28], in_=sm[:, b, 26:27])),
                  reads=[("sm", b, 5)], writes=[("sm", b, 6)])
            S.add("dve", (lambda e, b=b: e.tensor_scalar(
                out=sm[:, b, 28:29], in0=sm[:, b, 24:25], scalar1=-1.0, scalar2=sm[:, b, 27:28],
                op0=ALU.mult, op1=ALU.mult)),
                  reads=[("sm", b, 4), ("sm", b, 6)], writes=[("sm", b, 7)])
            S.add("act", (lambda e, b=b: e.activation(
                out=x_tok[:, b, :], in_=x_tok[:, b, :], func=AF.Identity, scale=sm[:, b, 27:28],
                bias=sm[:, b, 28:29])),
                  reads=[("x", b), ("sm", b, 6), ("sm", b, 7)], writes=[("x", b)])
            S.add("pool", (lambda e, b=b: e.tensor_tensor(
                out=x_tok[:, b, :], in0=x_tok[:, b, :], in1=U[:, 0:4, :].rearrange("p a q -> p (a q)"),
                op=ALU.mult)),
                  reads=[("x", b)] + [("U", c) for c in range(4)], writes=[("x", b)])
            S.add("pool", (lambda e, b=b: e.tensor_tensor(
                out=x_tok[:, b, :], in0=x_tok[:, b, :], in1=U[:, 4:8, :].rearrange("p a q -> p (a q)"),
                op=ALU.add)),
                  reads=[("x", b)] + [("U", c) for c in range(4, 8)], writes=[("x", b)])
            if last:
                row = ((t - main_from) * NBLK + b) * 128
                out_ops.append(S.add("sp", (lambda e, b=b, row=row: e.dma_start(
                    out=y_out[row:row + 128, :], in_=x_tok[:, b, :])),
                    reads=[("x", b)], dma=("xo", b)))
            else:
                prep(b)

    out_ops = []
    for t in range(NT):
        for b in range(NBLK):
            row = (t * NBLK + b) * 128
            S.add("sp", (lambda e, b=b, row=row: e.dma_start(out=x_tok[:, b, :], in_=x_in[row:row + 128, :])),
                  writes=[("x", b)], dma=("xi", b))
            prep(b)
        for l in range(L):
            full = not (t < main_from and l == L - 1)
            tile_layer(t, l, full)
    fin = S.add("sp", lambda e: None)
    fin.deps = list(out_ops)

    S.finalize()
    dma_keys = list(S.dma_cnt.keys())
    eng_sems = {}
    for en in S.ENGS:
        n = max(1, -(-S.nticks[en] // EPOCH))
        eng_sems[en] = [es.enter_context(nc.semaphore(f"s_{en}{i}")) for i in range(n)]
    dma_sems = {k: es.enter_context(nc.semaphore("d_" + "_".join(str(x) for x in k))) for k in dma_keys}
    with nc.Block() as block:
        @block.tensor
        def _(e):
            S.emit("pe", e, eng_sems, dma_sems)

        @block.scalar
        def _(e):
            S.emit("act", e, eng_sems, dma_sems)

        @block.vector
        def _(e):
            S.emit("dve", e, eng_sems, dma_sems)

        @block.gpsimd
        def _(e):
            S.emit("pool", e, eng_sems, dma_sems)

        @block.sync
        def _(e):
            S.emit("sp", e, eng_sems, dma_sems)
    es.close()
    return nc


def prep_params(L, w_in, b_in, conv_w, conv_b, conv_ln_g, conv_ln_b, sinks, w_out, b_out,
                ln_g, ln_b):
    f = np.float32
    wall = np.empty((L * NG, 128, 8192), dtype=f)
    pp = np.zeros((128, L * NPP), dtype=f)
    sk = np.zeros((4, L * 4), dtype=f)
    bo = np.zeros((4, L * 512), dtype=f)
    bv = np.zeros((1, L * 128), dtype=f)
    for l in range(L):
        cols = np.concatenate([_chunk_cols(k, i) for (k, i) in IN_CHUNKS])
        wi = np.asarray(w_in[l])[:, cols]
        wi = wi.reshape(KC, 128, NG_IN, 512).transpose(2, 1, 0, 3)
        wall[l * NG:l * NG + NG_IN] = wi.reshape(NG_IN, 128, 8192)
        wo = np.asarray(w_out[l]).reshape(KC, 128, 4, 512).transpose(2, 1, 0, 3)
        wall[l * NG + NG_IN:(l + 1) * NG] = wo.reshape(4, 128, 8192)
        bi = np.asarray(b_in[l])[cols].reshape(44, 128).T
        pp[:, l * NPP:l * NPP + 44] = bi
        cw = np.asarray(conv_w[l]).reshape(CW, 8, 128).transpose(2, 1, 0)
        pp[:, l * NPP + PP_CW:l * NPP + PP_CB] = cw.reshape(128, 8 * CW)
        pp[:, l * NPP + PP_CB:l * NPP + PP_LG] = np.asarray(conv_b[l]).reshape(8, 128).T
        pp[:, l * NPP + PP_LG:l * NPP + PP_LB] = np.asarray(conv_ln_g[l]).reshape(8, 128).T
        pp[:, l * NPP + PP_LB:l * NPP + PP_LB + 8] = np.asarray(conv_ln_b[l]).reshape(8, 128).T
        sl = np.asarray(sinks[l])
        for par in range(2):
            for u in range(2):
                for k in range(4):
                    sk[par * 2 + u, l * 4 + k] = sl[8 * u + 2 * k + par]
        bo[:, l * 512:(l + 1) * 512] = np.asarray(b_out[l]).reshape(4, 512)
        bv[0, l * 128:(l + 1) * 128] = np.asarray(b_in[l])[4224:4352]
    return dict(wall=wall, pp_in=pp, sk_in=sk, bo_in=bo, bv_in=bv,
                lng_in=np.ascontiguousarray(np.asarray(ln_g)[:L], dtype=f),
                lnb_in=np.ascontiguousarray(np.asarray(ln_b)[:L], dtype=f))


def prep_consts(m):
    f = np.float32
    cst = np.zeros((128, NCST), dtype=f)
    s = np.arange(128)[:, None]
    q = np.arange(128)[None, :]
    cst[:, C_MA:C_MA + 128] = (q >= s)
    cst[:, C_MB:C_MB + 128] = (q < s)
    cst[:, C_MB0:C_MB0 + 128] = (q < s) * float(m)
    cst[:, C_ID:C_ID + 128] = np.eye(128)
    cst[:, C_ONES:C_ONES + 128] = 1.0
    for dg in range(4):
        cst[dg, C_OHBO + dg * 128:C_OHBO + (dg + 1) * 128] = 1.0
    for par in range(2):
        for u in range(2):
            r = par * 2 + u
            lo = (1 - par) * 64
            cst[r, C_OHSK + r * 128 + lo:C_OHSK + r * 128 + lo + 64] = 1.0
    mcol = np.full((128, 1), float(m), dtype=f)
    return cst, mcol


_NC_CACHE = {}


def kernel(x, w_in, b_in, conv_w, conv_b, conv_ln_g, conv_ln_b, sinks, w_out, b_out, ln_g, ln_b):
    L, NT = 4, 5
    x = np.asarray(x, dtype=np.float32)
    B, SEQ, _ = x.shape
    params = prep_params(L, w_in, b_in, conv_w, conv_b, conv_ln_g, conv_ln_b, sinks, w_out,
                         b_out, ln_g, ln_b)
    key = (L, NT)
    if key not in _NC_CACHE:
        _NC_CACHE[key] = build_nc(L, NT)
    nc = _NC_CACHE[key]
    in_maps = []
    for c in range(8):
        b, half = c // 2, c % 2
        t0 = half * 2048
        xin = np.zeros((NT * TT, D), dtype=np.float32)
        if half == 0:
            xin[TT:] = x[b, 0:2048]
        else:
            xin[:] = x[b, t0 - TT:t0 + 2048]
        cst, mcol = prep_consts(half)
        mp = dict(params)
        mp.update(x_in=xin, cst_in=cst, m_in=mcol)
        in_maps.append(mp)
    res = run_bass_kernel_spmd(nc, in_maps, core_ids=list(range(8)))
    out = np.empty((B, SEQ, D), dtype=np.float32)
    for c in range(8):
        b, half = c // 2, c % 2
        out[b, half * 2048:(half + 1) * 2048] = res.results[c]["y"]
    return out
```

```python
from contextlib import ExitStack

import numpy as np
import concourse.bass as bass
import concourse.mybir as mybir
from concourse.bass_utils import run_bass_kernel_spmd

F32 = mybir.dt.float32
BF16 = mybir.dt.bfloat16
AF = mybir.ActivationFunctionType
ALU = mybir.AluOpType

D = 2048
CC = 1024
TT = 512
NBLK = 4
KC = 16
CW = 31
PRE = 30
HB_W = 544
ALPHA = float(8 ** 0.25)
EPS = 1e-5
NG_IN = 11
NG = 15
NPP = 44 + 8 * CW + 24
PP_CW = 44
PP_CB = 44 + 8 * CW
PP_LG = PP_CB + 8
PP_LB = PP_LG + 8
C_MA, C_MB, C_MB0, C_ID, C_ONES, C_OHBO, C_OHSK = 0, 128, 256, 384, 512, 640, 1152
NCST = 1664
EPOCH = 8000
LOOKBACK = 6
NWR = 2
NTF = 8
NTB = 4
NPT = 4
NXBF = 2
NDG = 3

def _in_chunks():
    ch = []
    for g in range(4):
        for c in (2 * g, 2 * g + 1):
            ch.append(("val", c))
            ch.append(("glu", c))
    ch += [("kd", 0), ("kd", 1), ("v", 0), ("pad", 0)]
    ch += [("q", i) for i in range(8)]
    ch += [("gA", i) for i in range(8)]
    ch += [("gB", i) for i in range(8)]
    return ch


IN_CHUNKS = _in_chunks()


def _chunk_cols(kind, i):
    if kind == "val":
        return np.arange(i * 128, (i + 1) * 128)
    if kind == "glu":
        return 1024 + np.arange(i * 128, (i + 1) * 128)
    if kind == "gA":
        return 2048 + np.arange(i * 128, (i + 1) * 128)
    if kind == "q":
        return 3072 + np.arange(i * 128, (i + 1) * 128)
    if kind == "kd":
        base = 4096 + i * 64 + np.arange(64)
        return np.concatenate([base, base])
    if kind == "v":
        return 4224 + np.arange(128)
    if kind == "gB":
        return 4352 + np.arange(i * 128, (i + 1) * 128)
    return np.zeros(128, dtype=np.int64)


class Op:
    __slots__ = ("eng", "fn", "deps", "idx", "dma", "tick", "needs_inc", "dval")


class Sched:
    ENGS = ("pe", "act", "dve", "pool", "sp")

    def __init__(self):
        self.ops = {e: [] for e in self.ENGS}
        self.lw = {}
        self.rd = {}
        self.dma_cnt = {}
        self.rr = {}

    def alloc(self, name, n):
        i = self.rr.get(name, 0)
        self.rr[name] = (i + 1) % n
        return i

    def add(self, eng, fn, reads=(), writes=(), dma=None):
        op = Op()
        op.eng = eng
        op.fn = fn
        op.dma = dma
        op.tick = 0
        op.needs_inc = False
        op.dval = 0
        deps = []
        for k in reads:
            w = self.lw.get(k)
            if w is not None:
                deps.append(w)
        for k in writes:
            w = self.lw.get(k)
            if w is not None:
                deps.append(w)
            deps.extend(self.rd.get(k, ()))
        seen = set()
        ud = []
        for d in deps:
            if d is op or id(d) in seen:
                continue
            seen.add(id(d))
            ud.append(d)
        op.deps = ud
        for k in reads:
            self.rd.setdefault(k, []).append(op)
        for k in writes:
            self.lw[k] = op
            self.rd[k] = []
        if dma is not None:
            c = self.dma_cnt.get(dma, 0) + 16
            self.dma_cnt[dma] = c
            op.dval = c
        op.idx = len(self.ops[eng])
        self.ops[eng].append(op)
        return op

    def finalize(self):
        for e in self.ENGS:
            for op in self.ops[e]:
                for d in op.deps:
                    if d.dma is None:
                        d.needs_inc = True
        self.nticks = {}
        for e in self.ENGS:
            t = 0
            for op in self.ops[e]:
                if op.dma is None and op.needs_inc:
                    t += 1
                    op.tick = t
            self.nticks[e] = t

    def emit(self, eng, e, eng_sems, dma_sems):
        seen = {}
        for op in self.ops[eng]:
            for d in op.deps:
                if d.dma is not None:
                    key = ("dma", d.dma)
                    if seen.get(key, 0) >= d.dval:
                        continue
                    seen[key] = d.dval
                    e.wait_ge(dma_sems[d.dma], d.dval)
                else:
                    if d.eng == eng and (eng == "pe" or d.idx < op.idx - LOOKBACK):
                        continue
                    if seen.get(d.eng, 0) >= d.tick:
                        continue
                    seen[d.eng] = d.tick
                    s, v = divmod(d.tick - 1, EPOCH)
                    e.wait_ge(eng_sems[d.eng][s], v + 1)
            ins = op.fn(e)
            if ins is None:
                continue
            if op.dma is not None:
                ins.then_inc(dma_sems[op.dma], 16)
            elif op.needs_inc:
                s, v = divmod(op.tick - 1, EPOCH)
                ins.then_inc(eng_sems[eng][s], 1)


def build_nc(L, NT, main_from=1, pool_conv=(), mask_eng="dve", gate_eng="pool"):
    nc = bass.Bass("TRN2", target_bir_lowering=False)
    NOUT = NT - main_from
    x_in = nc.dram_tensor("x_in", [NT * TT, D], F32, kind="ExternalInput").ap()
    wall = nc.dram_tensor("wall", [L * NG, 128, 8192], F32, kind="ExternalInput").ap()
    pp_in = nc.dram_tensor("pp_in", [128, L * NPP], F32, kind="ExternalInput").ap()
    cst_in = nc.dram_tensor("cst_in", [128, NCST], F32, kind="ExternalInput").ap()
    sk_in = nc.dram_tensor("sk_in", [4, L * 4], F32, kind="ExternalInput").ap()
    bo_in = nc.dram_tensor("bo_in", [4, L * 512], F32, kind="ExternalInput").ap()
    bv_in = nc.dram_tensor("bv_in", [1, L * 128], F32, kind="ExternalInput").ap()
    lng_in = nc.dram_tensor("lng_in", [L, D], F32, kind="ExternalInput").ap()
    lnb_in = nc.dram_tensor("lnb_in", [L, D], F32, kind="ExternalInput").ap()
    m_in = nc.dram_tensor("m_in", [128, 1], F32, kind="ExternalInput").ap()
    y_out = nc.dram_tensor("y", [NOUT * TT, D], F32, kind="ExternalOutput").ap()
    wbf = nc.dram_tensor("wbf", [L * NG, 128, 8192], BF16).ap()

    S = Sched()
    es = ExitStack()

    def sb(name, shape, dt):
        return es.enter_context(nc.sbuf_tensor(name, shape, dt))

    x_tok = sb("x_tok", [128, NBLK, D], F32)
    xT = sb("xT", [128, KC, TT], BF16)
    hb = sb("hb", [128, 8, HB_W], BF16)
    gA = sb("gA", [128, 8, TT], BF16)
    gB = sb("gB", [128, 8, TT], BF16)
    qT = sb("qT", [128, 8, TT], BF16)
    kT = sb("kT", [128, 2, 640], BF16)
    vaug = sb("vaug", [128, 5, 4, 128], BF16)
    U = sb("U", [128, 8, TT], F32)
    pT = [sb(f"pT{i}", [128, 2, TT], BF16) for i in range(NPT)]
    xbf = [sb(f"xbf{i}", [128, D], BF16) for i in range(NXBF)]
    tf = [sb(f"tf{i}", [128, TT], F32) for i in range(NTF)]
    tb = [sb(f"tb{i}", [128, TT], BF16) for i in range(NTB)]
    wr = [sb(f"wr{i}", [128, KC, TT], BF16) for i in range(NWR)]
    pp = sb("pp", [128, L * NPP], F32)
    cst = sb("cst", [128, NCST], BF16)
    skf = sb("skf", [4, L * 4], F32)
    ske = sb("ske", [4, L * 4], F32)
    sk4 = sb("sk4", [4, L * 4, 128], BF16)
    bo4 = sb("bo4", [4, L * 512], BF16)
    bvbc = sb("bvbc", [128, L * 128], F32)
    mcol = sb("mcol", [128, 1], F32)
    hst = sb("hst", [128, L, 8, PRE], BF16)
    kst = sb("kst", [128, L, 2, 128], BF16)
    vst = sb("vst", [128, L, 4, 128], BF16)
    sm = sb("sm", [128, NBLK, 32], F32)
    dgb = [sb(f"dgb{i}", [128, 8, 128], BF16) for i in range(NDG)]
    ps = [es.enter_context(nc.psum_tensor(f"ps{i}", [128, 512], F32)) for i in range(8)]

    def ppc(l, off):
        return pp[:, l * NPP + off:l * NPP + off + 1]

    def cs(off, n=128, rows=None):
        if rows is None:
            return cst[:, off:off + n]
        return cst[0:rows, off:off + n]

    def psb():
        return S.alloc("ps", 8)

    S.add("sp", lambda e: e.dma_start(out=pp[:, :], in_=pp_in), writes=[("pp",)], dma=("su", 0))
    S.add("sp", lambda e: e.dma_start(out=skf[:, :], in_=sk_in), writes=[("skf",)], dma=("su", 1))
    S.add("sp", lambda e: e.dma_start(out=mcol[:, :], in_=m_in), writes=[("mcol",)], dma=("su", 2))
    S.add("sp", lambda e: e.dma_start(out=bvbc[:, :], in_=bv_in.broadcast_to([128, L * 128])),
          writes=[("bvbc",)], dma=("su", 3))
    S.add("pool", lambda e: e.dma_start(out=cst[:, :], in_=cst_in), writes=[("cst",)], dma=("su", 4))
    S.add("pool", lambda e: e.dma_start(out=bo4[:, :], in_=bo_in), writes=[("bo4",)], dma=("su", 5))
    cast_ops = {}
    for l in range(L):
        for g in range(NG):
            i = l * NG + g
            grp = ("wc", i) if l == 0 else ("wcl", l)
            cast_ops[(l, g)] = S.add(
                "pool", (lambda e, i=i: e.dma_start(out=wbf[i], in_=wall[i])),
                writes=[("wbf", l, g)], dma=grp)
        if l > 0:
            for g in range(NG):
                cast_ops[(l, g)].dval = 16 * NG
    S.add("pool", lambda e: e.memset(vaug[:, :, :, :], 1.0), writes=[("v", b) for b in range(5)])
    S.add("pool", lambda e: e.memset(vst[:, :, :, :], 1.0), writes=[("vst", l) for l in range(L)])
    S.add("pool", lambda e: e.memset(kst[:, :, :, :], 0.0), writes=[("kst", l) for l in range(L)])
    S.add("pool", lambda e: e.memset(hst[:, :, :, :], 0.0), writes=[("hst", l) for l in range(L)])
    S.add("pool", lambda e: e.memset(hb[:, :, :], 0.0), writes=[("h", c) for c in range(8)])
    S.add("act", lambda e: e.activation(out=ske[:, :], in_=skf[:, :], func=AF.Exp),
          reads=[("skf",)], writes=[("ske",)])
    S.add("dve", lambda e: e.tensor_copy(
        out=sk4[:, :, :], in_=ske[:, :].unsqueeze(2).broadcast_to([4, L * 4, 128])),
          reads=[("ske",)], writes=[("sk4",)])

    def load_w(l, g):
        slot = S.alloc("wr", NWR)
        i = l * NG + g
        S.add("sp", (lambda e, i=i, slot=slot: e.dma_start(
            out=wr[slot][:, :, :], in_=wbf[i].rearrange("p (k e) -> p k e", k=KC))),
              reads=[("wbf", l, g)], writes=[("w", slot)], dma=("wr", slot))
        return slot

    def prep(b):
        xi = S.alloc("xbf", NXBF)
        S.add("pool", (lambda e, b=b, xi=xi: e.tensor_copy(out=xbf[xi][:, :], in_=x_tok[:, b, :])),
              reads=[("x", b)], writes=[("xbf", xi)])
        for hlf in range(2):
            bank = psb()
            pv = ps[bank].bitcast(BF16)

            def tfn(e, hlf=hlf, xi=xi, pv=pv):
                ins = None
                for j in range(8):
                    kc = hlf * 8 + j
                    ins = e.transpose(out=pv[:, j * 128:(j + 1) * 128],
                                      in_=xbf[xi][:, kc * 128:(kc + 1) * 128],
                                      identity=cs(C_ID))
                return ins
            S.add("pe", tfn, reads=[("xbf", xi), ("cst",)], writes=[("ps", bank)])
            eng = "act" if hlf == 0 else "dve"

            def efn(e, hlf=hlf, b=b, pv=pv, eng=eng):
                o = xT[:, hlf * 8:(hlf + 1) * 8, b * 128:(b + 1) * 128]
                i_ = pv[:, 0:1024].rearrange("p (a q) -> p a q", a=8)
                if eng == "act":
                    return e.activation(out=o, in_=i_, func=AF.Copy)
                return e.tensor_copy(out=o, in_=i_)
            S.add(eng, efn, reads=[("ps", bank)], writes=[("xT", b, hlf)])

    XT_KEYS = [("xT", b, h) for b in range(NBLK) for h in range(2)]

    def conv_chunk(l, c, eng):
        for k in range(CW):
            def fn(e, k=k, c=c, l=l):
                src = hb[:, c, k:k + TT]
                wk = ppc(l, PP_CW + c * CW + k)
                if k == 0:
                    return e.tensor_scalar(out=U[:, c, :], in0=src, scalar1=wk,
                                           scalar2=ppc(l, PP_CB + c), op0=ALU.mult, op1=ALU.add)
                if eng == "dve":
                    return e.scalar_tensor_tensor(out=U[:, c, :], in0=src, scalar=wk,
                                                  in1=U[:, c, :], op0=ALU.mult, op1=ALU.add)
                return None
            if eng == "dve" or k == 0:
                S.add(eng, fn, reads=[("h", c), ("pp",), ("U", c)] if k else [("h", c), ("pp",)],
                      writes=[("U", c)])
            else:
                ti = S.alloc("tf", NTF)
                S.add("pool", (lambda e, k=k, c=c, l=l, ti=ti: e.tensor_scalar(
                    out=tf[ti][:, :], in0=hb[:, c, k:k + TT], scalar1=ppc(l, PP_CW + c * CW + k),
                    scalar2=None, op0=ALU.mult)),
                      reads=[("h", c), ("pp",)], writes=[("tf", ti)])
                S.add("pool", (lambda e, c=c, ti=ti: e.tensor_tensor(
                    out=U[:, c, :], in0=U[:, c, :], in1=tf[ti][:, :], op=ALU.add)),
                      reads=[("tf", ti), ("U", c)], writes=[("U", c)])

    def conv_pe(l, c):
        bank = psb()
        for g0 in range(0, CW, 8):
            n = min(8, CW - g0)
            di = S.alloc("dg", NDG)

            def dfn(e, g0=g0, n=n, di=di):
                ins = None
                for j in range(n):
                    ins = e.activation(out=dgb[di][:, j, :], in_=cs(C_ID), func=AF.Identity,
                                       scale=ppc(l, PP_CW + c * CW + g0 + j))
                return ins
            S.add("act", dfn, reads=[("cst",), ("pp",)], writes=[("dg", di)])

            def mfn(e, g0=g0, n=n, di=di):
                ins = None
                for j in range(n):
                    k = g0 + j
                    ins = e.matmul(ps[bank][:, :], lhsT=dgb[di][:, j, :], rhs=hb[:, c, k:k + TT],
                                   start=(k == 0), stop=(k == CW - 1))
                return ins
            S.add("pe", mfn, reads=[("dg", di), ("h", c)], writes=[("ps", bank)])
        S.add("act", lambda e: e.activation(out=U[:, c, :], in_=ps[bank][:, :], func=AF.Identity,
                                            bias=ppc(l, PP_CB + c)),
              reads=[("ps", bank), ("pp",)], writes=[("U", c)])

    def tile_layer(t, l, full):
        S.add("pool", lambda e: e.tensor_copy(out=hb[:, :, 0:PRE], in_=hst[:, l, :, :]),
              reads=[("hst", l)], writes=[("h", c) for c in range(8)])
        S.add("pool", lambda e: e.tensor_copy(out=kT[:, :, 0:128], in_=kst[:, l, :, :]),
              reads=[("kst", l)], writes=[("k", 0), ("k", 1)])
        S.add("pool", lambda e: e.tensor_copy(out=vaug[:, 0, :, :], in_=vst[:, l, :, :]),
              reads=[("vst", l)], writes=[("v", 0)])

        def proj_fn(bank, slot, j):
            def fn(e):
                ins = None
                for kc in range(KC):
                    ins = e.matmul(ps[bank][:, :], lhsT=wr[slot][:, kc, j * 128:(j + 1) * 128],
                                   rhs=xT[:, kc, :], start=(kc == 0), stop=(kc == KC - 1))
                return ins
            return fn

        ngroups = NG_IN if full else 5
        pending = []
        for g in range(ngroups):
            ready = list(pending)
            del pending[:]
            slot = load_w(l, g)
            for j in range(4):
                ci = g * 4 + j
                kind, idx = IN_CHUNKS[ci]
                bcol = ppc(l, ci)
                if kind == "pad":
                    continue
                if kind == "v":
                    bank = psb()

                    def vfn(e, bank=bank, slot=slot, j=j):
                        ins = None
                        for b in range(NBLK):
                            for kc in range(KC):
                                ins = e.matmul(ps[bank][:, b * 128:(b + 1) * 128],
                                               lhsT=xT[:, kc, b * 128:(b + 1) * 128],
                                               rhs=wr[slot][:, kc, j * 128:(j + 1) * 128],
                                               start=(kc == 0), stop=(kc == KC - 1))
                        return ins
                    S.add("pe", vfn, reads=[("w", slot)] + XT_KEYS, writes=[("ps", bank)])
                    for b in range(NBLK):
                        for par in range(2):
                            def vev(e, b=b, par=par, bank=bank):
                                o = vaug[:, 1 + b, par:4:2, par * 64:par * 64 + 64]
                                i0 = ps[bank][:, b * 128:(b + 1) * 128].rearrange("p (u d) -> p u d", u=2)
                                i1 = bvbc[:, l * 128:(l + 1) * 128].rearrange("p (u d) -> p u d", u=2)
                                return e.tensor_tensor(out=o, in0=i0, in1=i1, op=ALU.add)
                            S.add("dve", vev, reads=[("ps", bank), ("bvbc",)], writes=[("v", 1 + b)])
                    continue
                if kind == "glu":
                    continue
                bank = psb()
                S.add("pe", proj_fn(bank, slot, j), reads=[("w", slot)] + XT_KEYS,
                      writes=[("ps", bank)])
                if kind == "val":
                    bank_g = psb()
                    S.add("pe", proj_fn(bank_g, slot, j + 1), reads=[("w", slot)] + XT_KEYS,
                          writes=[("ps", bank_g)])
                    ti = S.alloc("tf", NTF)
                    bg = ppc(l, ci + 1)
                    S.add("act", (lambda e, bank_g=bank_g, ti=ti, bg=bg: e.activation(
                        out=tf[ti][:, :], in_=ps[bank_g][:, :], func=AF.Sigmoid, bias=bg)),
                          reads=[("ps", bank_g), ("pp",)], writes=[("tf", ti)])
                    S.add("dve", (lambda e, bank=bank, ti=ti, idx=idx, bcol=bcol: e.scalar_tensor_tensor(
                        out=hb[:, idx, PRE:PRE + TT], in0=ps[bank][:, :], scalar=bcol,
                        in1=tf[ti][:, :], op0=ALU.add, op1=ALU.mult)),
                          reads=[("ps", bank), ("tf", ti), ("pp",)], writes=[("h", idx)])
                    if full:
                        pending.append(idx)
                    continue
                if kind == "kd":
                    S.add("act", (lambda e, bank=bank, idx=idx, bcol=bcol: e.activation(
                        out=kT[:, idx, 128:640], in_=ps[bank][:, :], func=AF.Identity, bias=bcol)),
                          reads=[("ps", bank), ("pp",)], writes=[("k", idx)])
                elif kind == "q":
                    S.add("act", (lambda e, bank=bank, idx=idx, bcol=bcol: e.activation(
                        out=qT[:, idx, :], in_=ps[bank][:, :], func=AF.Identity, bias=bcol)),
                          reads=[("ps", bank), ("pp",)], writes=[("q", idx)])
                elif kind in ("gA", "gB"):
                    dst = gA if kind == "gA" else gB
                    S.add("act", (lambda e, bank=bank, idx=idx, bcol=bcol, dst=dst: e.activation(
                        out=dst[:, idx, :], in_=ps[bank][:, :], func=AF.Silu, bias=bcol)),
                          reads=[("ps", bank), ("pp",)], writes=[(kind, idx)])

            for c_ in ready:
                conv_pe(l, c_)
        for c_ in pending:
            conv_pe(l, c_)

        if t == 0:
            S.add("pool", lambda e: e.tensor_scalar(
                out=hst[:, l, :, :], in0=hb[:, :, TT:TT + PRE], scalar1=mcol[:, 0:1], scalar2=None,
                op0=ALU.mult),
                  reads=[("h", c) for c in range(8)] + [("mcol",)], writes=[("hst", l)])
        else:
            S.add("pool", lambda e: e.tensor_copy(out=hst[:, l, :, :], in_=hb[:, :, TT:TT + PRE]),
                  reads=[("h", c) for c in range(8)], writes=[("hst", l)])
        S.add("pool", lambda e: e.tensor_copy(out=kst[:, l, :, :], in_=kT[:, :, TT:TT + 128]),
              reads=[("k", 0), ("k", 1)], writes=[("kst", l)])
        S.add("pool", lambda e: e.tensor_copy(out=vst[:, l, :, :], in_=vaug[:, 4, :, :]),
              reads=[("v", 4)], writes=[("vst", l)])
        if not full:
            return

        iters = [(qb, u, par) for qb in range(NBLK) for u in range(2) for par in range(2)]
        st = {}

        def att_front(i):
            qb, u, par = iters[i]
            pr = slice(par * 64, par * 64 + 64)
            pslot = S.alloc("pT", NPT)
            banks = []
            for kb in range(2):
                bank = psb()
                banks.append(bank)
                k0 = (qb + kb) * 128

                def sfn(e, bank=bank, k0=k0, u=u, pr=pr, qb=qb):
                    return e.matmul(
                        ps[bank][:, :].rearrange("p (a q) -> p a q", a=4),
                        lhsT=kT[pr, u, k0:k0 + 128],
                        rhs=qT[pr, 4 * u:4 * u + 4, qb * 128:(qb + 1) * 128],
                        start=True, stop=True)
                S.add("pe", sfn, reads=[("k", u)] + [("q", 4 * u + a) for a in range(4)],
                      writes=[("ps", bank)])
            for kb in range(2):
                S.add("act", (lambda e, bank=banks[kb], pslot=pslot, kb=kb: e.activation(
                    out=pT[pslot][:, kb, :], in_=ps[bank][:, :], func=AF.Exp, scale=0.125)),
                      reads=[("ps", banks[kb])], writes=[("pT", pslot, kb)])
                if kb == 0:
                    moff = C_MB0 if (t == main_from and qb == 0 and main_from > 0) else C_MB
                else:
                    moff = C_MA

                def mfn(e, pslot=pslot, kb=kb, moff=moff):
                    v_ = pT[pslot][:, kb, :].rearrange("p (a q) -> p a q", a=4)
                    return e.tensor_tensor(
                        out=v_, in0=v_,
                        in1=cs(moff).unsqueeze(1).broadcast_to([128, 4, 128]),
                        op=ALU.mult)
                S.add(mask_eng, mfn, reads=[("pT", pslot, kb), ("cst",)],
                      writes=[("pT", pslot, kb)])
            st[i] = pslot

        def att_back(i):
            qb, u, par = iters[i]
            pslot = st[i]
            orow = slice(par * 64, par * 64 + 64)
            drow = slice((1 - par) * 64, (1 - par) * 64 + 64)
            r = par * 2 + u
            var = u * 2 + par
            bank = psb()

            def pvfn(e):
                e.matmul(ps[bank][:, :], lhsT=vaug[:, qb, var, :], rhs=pT[pslot][:, 0, :],
                         start=True, stop=False)
                e.matmul(ps[bank][:, :], lhsT=vaug[:, qb + 1, var, :], rhs=pT[pslot][:, 1, :],
                         start=False, stop=False)
                return e.matmul(ps[bank][:, :], lhsT=cs(C_OHSK + r * 128, 128, rows=4),
                                rhs=sk4[:, l * 4:(l + 1) * 4, :], start=False, stop=True)
            S.add("pe", pvfn, reads=[("v", qb), ("v", qb + 1), ("pT", pslot, 0),
                                     ("pT", pslot, 1), ("cst",), ("sk4",)],
                  writes=[("ps", bank)])
            t1 = S.alloc("tf", NTF)
            S.add("dve", lambda e: e.reciprocal(out=tf[t1][orow, :], in_=ps[bank][drow, :]),
                  reads=[("ps", bank)], writes=[("tf", t1)])
            t2 = S.alloc("tf", NTF)
            S.add("dve", lambda e: e.tensor_tensor(
                out=tf[t2][orow, :], in0=ps[bank][orow, :], in1=tf[t1][orow, :], op=ALU.mult),
                  reads=[("ps", bank), ("tf", t1)], writes=[("tf", t2)])

            def gfn(e):
                o = gB[orow, 4 * u:4 * u + 4, qb * 128:(qb + 1) * 128]
                return e.tensor_tensor(
                    out=o, in0=o, in1=tf[t2][orow, :].rearrange("p (a q) -> p a q", a=4),
                    op=ALU.mult)
            S.add(gate_eng, gfn, reads=[("tf", t2)] + [("gB", 4 * u + a) for a in range(4)],
                  writes=[("gB", 4 * u + a) for a in range(4)])

        DEPTH_ATT = 2
        for i in range(min(DEPTH_ATT, len(iters))):
            att_front(i)
        for i in range(len(iters)):
            att_back(i)
            if i + DEPTH_ATT < len(iters):
                att_front(i + DEPTH_ATT)

        b1 = psb()
        b2 = psb()
        for c in range(8):
            i1 = S.alloc("tb", NTB)
            S.add("act", (lambda e, c=c, i1=i1: e.activation(out=tb[i1][:, :], in_=U[:, c, :], func=AF.Copy)),
                  reads=[("U", c)], writes=[("tb", i1)])
            i2 = S.alloc("tb", NTB)
            S.add("act", (lambda e, c=c, i2=i2: e.activation(out=tb[i2][:, :], in_=U[:, c, :], func=AF.Square)),
                  reads=[("U", c)], writes=[("tb", i2)])

            def stf(e, c=c, i1=i1, i2=i2):
                e.matmul(ps[b1][:, :], lhsT=cs(C_ONES), rhs=tb[i1][:, :], start=(c == 0), stop=(c == 7))
                return e.matmul(ps[b2][:, :], lhsT=cs(C_ONES), rhs=tb[i2][:, :], start=(c == 0), stop=(c == 7))
            S.add("pe", stf, reads=[("tb", i1), ("tb", i2), ("cst",)], writes=[("ps", b1), ("ps", b2)])
        imu = S.alloc("tf", NTF)
        S.add("dve", lambda e: e.tensor_scalar(out=tf[imu][:, :], in0=ps[b1][:, :], scalar1=1.0 / CC,
                                               scalar2=None, op0=ALU.mult),
              reads=[("ps", b1)], writes=[("tf", imu)])
        imq = S.alloc("tf", NTF)
        S.add("dve", lambda e: e.tensor_tensor(out=tf[imq][:, :], in0=tf[imu][:, :], in1=tf[imu][:, :],
                                               op=ALU.mult),
              reads=[("tf", imu)], writes=[("tf", imq)])
        ivar = S.alloc("tf", NTF)
        S.add("dve", lambda e: e.scalar_tensor_tensor(out=tf[ivar][:, :], in0=ps[b2][:, :], scalar=1.0 / CC,
                                                      in1=tf[imq][:, :], op0=ALU.mult, op1=ALU.subtract),
              reads=[("ps", b2), ("tf", imq)], writes=[("tf", ivar)])
        S.add("act", lambda e: e.activation(out=tf[imq][:, :], in_=tf[ivar][:, :], func=AF.Sqrt, bias=EPS),
              reads=[("tf", ivar)], writes=[("tf", imq)])
        S.add("dve", lambda e: e.reciprocal(out=tf[ivar][:, :], in_=tf[imq][:, :]),
              reads=[("tf", imq)], writes=[("tf", ivar)])
        irs = ivar
        for c in range(8):
            S.add("dve", (lambda e, c=c: e.tensor_tensor(out=U[:, c, :], in0=U[:, c, :], in1=tf[imu][:, :],
                                                         op=ALU.subtract)),
                  reads=[("U", c), ("tf", imu)], writes=[("U", c)])
            S.add("dve", (lambda e, c=c: e.tensor_tensor(out=U[:, c, :], in0=U[:, c, :], in1=tf[irs][:, :],
                                                         op=ALU.mult)),
                  reads=[("U", c), ("tf", irs)], writes=[("U", c)])
            i1 = S.alloc("tb", NTB)
            S.add("act", (lambda e, c=c, i1=i1: e.activation(
                out=tb[i1][:, :], in_=U[:, c, :], func=AF.Silu, scale=ppc(l, PP_LG + c),
                bias=ppc(l, PP_LB + c))),
                  reads=[("U", c), ("pp",)], writes=[("tb", i1)])
            S.add(gate_eng, (lambda e, c=c, i1=i1: e.tensor_tensor(
                out=gA[:, c, :], in0=gA[:, c, :], in1=tb[i1][:, :], op=ALU.mult)),
                  reads=[("tb", i1), ("gA", c)], writes=[("gA", c)])
        S.add("sp", lambda e: e.dma_start(out=U[:, 0:4, :].rearrange("p a q -> p (a q)"),
                                          in_=lng_in[l:l + 1, :].broadcast_to([128, D])),
              writes=[("U", c) for c in range(4)], dma=("lg", 0))
        S.add("sp", lambda e: e.dma_start(out=U[:, 4:8, :].rearrange("p a q -> p (a q)"),
                                          in_=lnb_in[l:l + 1, :].broadcast_to([128, D])),
              writes=[("U", c) for c in range(4, 8)], dma=("lg", 1))

        for dg in range(4):
            slot = load_w(l, NG_IN + dg)
            for b in range(NBLK):
                bank = psb()

                def ofn(e, bank=bank, slot=slot, b=b, dg=dg):
                    for ec in range(16):
                        ysrc = gA if ec < 8 else gB
                        e.matmul(ps[bank][:, :], lhsT=ysrc[:, ec % 8, b * 128:(b + 1) * 128],
                                 rhs=wr[slot][:, ec, :], start=(ec == 0), stop=False)
                    return e.matmul(ps[bank][:, :], lhsT=cs(C_OHBO + dg * 128, 128, rows=4),
                                    rhs=bo4[:, l * 512:(l + 1) * 512], start=False, stop=True)
                S.add("pe", ofn, reads=[("w", slot), ("cst",), ("bo4",)] +
                      [("gA", c) for c in range(8)] + [("gB", c) for c in range(8)],
                      writes=[("ps", bank)])
                S.add("dve", (lambda e, bank=bank, b=b, dg=dg: e.scalar_tensor_tensor(
                    out=x_tok[:, b, dg * 512:(dg + 1) * 512], in0=x_tok[:, b, dg * 512:(dg + 1) * 512],
                    scalar=ALPHA, in1=ps[bank][:, :], op0=ALU.mult, op1=ALU.add)),
                      reads=[("ps", bank), ("x", b)], writes=[("x", b)])
        last = (l == L - 1)
        for b in range(NBLK):
            for dg in range(4):
                S.add("dve", (lambda e, b=b, dg=dg: e.bn_stats(
                    out=sm[:, b, dg * 6:(dg + 1) * 6], in_=x_tok[:, b, dg * 512:(dg + 1) * 512])),
                      reads=[("x", b)], writes=[("sm", b, dg)])
            S.add("dve", (lambda e, b=b: e.bn_aggr(out=sm[:, b, 24:26], in_=sm[:, b, 0:24])),
                  reads=[("sm", b, dg) for dg in range(4)], writes=[("sm", b, 4)])
            S.add("act", (lambda e, b=b: e.activation(out=sm[:, b, 26:27], in_=sm[:, b, 25:26],
                                                      func=AF.Sqrt, bias=EPS)),
                  reads=[("sm", b, 4)], writes=[("sm", b, 5)])
            S.add("dve", (lambda e, b=b: e.reciprocal(out=sm[:, b, 27:28], in_=sm[:, b, 26:27])),
                  reads=[("sm", b, 5)], writes=[("sm", b, 6)])
            S.add("dve", (lambda e, b=b: e.tensor_scalar(
                out=sm[:, b, 28:29], in0=sm[:, b, 24:25], scalar1=-1.0, scalar2=sm[:, b, 27:28],
                op0=ALU.mult, op1=ALU.mult)),
                  reads=[("sm", b, 4), ("sm", b, 6)], writes=[("sm", b, 7)])
            S.add("act", (lambda e, b=b: e.activation(
                out=x_tok[:, b, :], in_=x_tok[:, b, :], func=AF.Identity, scale=sm[:, b, 27:28],
                bias=sm[:, b, 28:29])),
                  reads=[("x", b), ("sm", b, 6), ("sm", b, 7)], writes=[("x", b)])
            S.add("pool", (lambda e, b=b: e.tensor_tensor(
                out=x_tok[:, b, :], in0=x_tok[:, b, :], in1=U[:, 0:4, :].rearrange("p a q -> p (a q)"),
                op=ALU.mult)),
                  reads=[("x", b)] + [("U", c) for c in range(4)], writes=[("x", b)])
            S.add("pool", (lambda e, b=b: e.tensor_tensor(
                out=x_tok[:, b, :], in0=x_tok[:, b, :], in1=U[:, 4:8, :].rearrange("p a q -> p (a q)"),
                op=ALU.add)),
                  reads=[("x", b)] + [("U", c) for c in range(4, 8)], writes=[("x", b)])
            if last:
                row = ((t - main_from) * NBLK + b) * 128
                out_ops.append(S.add("sp", (lambda e, b=b, row=row: e.dma_start(
                    out=y_out[row:row + 128, :], in_=x_tok[:, b, :])),
                    reads=[("x", b)], dma=("xo", b)))
            else:
                prep(b)

    out_ops = []
    for t in range(NT):
        for b in range(NBLK):
            row = (t * NBLK + b) * 128
            S.add("sp", (lambda e, b=b, row=row: e.dma_start(out=x_tok[:, b, :], in_=x_in[row:row + 128, :])),
                  writes=[("x", b)], dma=("xi", b))
            prep(b)
        for l in range(L):
            full = not (t < main_from and l == L - 1)
            tile_layer(t, l, full)
    fin = S.add("sp", lambda e: None)
    fin.deps = list(out_ops)

    S.finalize()
    dma_keys = list(S.dma_cnt.keys())
    eng_sems = {}
    for en in S.ENGS:
        n = max(1, -(-S.nticks[en] // EPOCH))
        eng_sems[en] = [es.enter_context(nc.semaphore(f"s_{en}{i}")) for i in range(n)]
    dma_sems = {k: es.enter_context(nc.semaphore("d_" + "_".join(str(x) for x in k))) for k in dma_keys}
    with nc.Block() as block:
        @block.tensor
        def _(e):
            S.emit("pe", e, eng_sems, dma_sems)

        @block.scalar
        def _(e):
            S.emit("act", e, eng_sems, dma_sems)

        @block.vector
        def _(e):
            S.emit("dve", e, eng_sems, dma_sems)

        @block.gpsimd
        def _(e):
            S.emit("pool", e, eng_sems, dma_sems)

        @block.sync
        def _(e):
            S.emit("sp", e, eng_sems, dma_sems)
    es.close()
    return nc


def prep_params(L, w_in, b_in, conv_w, conv_b, conv_ln_g, conv_ln_b, sinks, w_out, b_out,
                ln_g, ln_b):
    f = np.float32
    wall = np.empty((L * NG, 128, 8192), dtype=f)
    pp = np.zeros((128, L * NPP), dtype=f)
    sk = np.zeros((4, L * 4), dtype=f)
    bo = np.zeros((4, L * 512), dtype=f)
    bv = np.zeros((1, L * 128), dtype=f)
    for l in range(L):
        cols = np.concatenate([_chunk_cols(k, i) for (k, i) in IN_CHUNKS])
        wi = np.asarray(w_in[l])[:, cols]
        wi = wi.reshape(KC, 128, NG_IN, 512).transpose(2, 1, 0, 3)
        wall[l * NG:l * NG + NG_IN] = wi.reshape(NG_IN, 128, 8192)
        wo = np.asarray(w_out[l]).reshape(KC, 128, 4, 512).transpose(2, 1, 0, 3)
        wall[l * NG + NG_IN:(l + 1) * NG] = wo.reshape(4, 128, 8192)
        bi = np.asarray(b_in[l])[cols].reshape(44, 128).T
        pp[:, l * NPP:l * NPP + 44] = bi
        cw = np.asarray(conv_w[l]).reshape(CW, 8, 128).transpose(2, 1, 0)
        pp[:, l * NPP + PP_CW:l * NPP + PP_CB] = cw.reshape(128, 8 * CW)
        pp[:, l * NPP + PP_CB:l * NPP + PP_LG] = np.asarray(conv_b[l]).reshape(8, 128).T
        pp[:, l * NPP + PP_LG:l * NPP + PP_LB] = np.asarray(conv_ln_g[l]).reshape(8, 128).T
        pp[:, l * NPP + PP_LB:l * NPP + PP_LB + 8] = np.asarray(conv_ln_b[l]).reshape(8, 128).T
        sl = np.asarray(sinks[l])
        for par in range(2):
            for u in range(2):
                for k in range(4):
                    sk[par * 2 + u, l * 4 + k] = sl[8 * u + 2 * k + par]
        bo[:, l * 512:(l + 1) * 512] = np.asarray(b_out[l]).reshape(4, 512)
        bv[0, l * 128:(l + 1) * 128] = np.asarray(b_in[l])[4224:4352]
    return dict(wall=wall, pp_in=pp, sk_in=sk, bo_in=bo, bv_in=bv,
                lng_in=np.ascontiguousarray(np.asarray(ln_g)[:L], dtype=f),
                lnb_in=np.ascontiguousarray(np.asarray(ln_b)[:L], dtype=f))


def prep_consts(m):
    f = np.float32
    cst = np.zeros((128, NCST), dtype=f)
    s = np.arange(128)[:, None]
    q = np.arange(128)[None, :]
    cst[:, C_MA:C_MA + 128] = (q >= s)
    cst[:, C_MB:C_MB + 128] = (q < s)
    cst[:, C_MB0:C_MB0 + 128] = (q < s) * float(m)
    cst[:, C_ID:C_ID + 128] = np.eye(128)
    cst[:, C_ONES:C_ONES + 128] = 1.0
    for dg in range(4):
        cst[dg, C_OHBO + dg * 128:C_OHBO + (dg + 1) * 128] = 1.0
    for par in range(2):
        for u in range(2):
            r = par * 2 + u
            lo = (1 - par) * 64
            cst[r, C_OHSK + r * 128 + lo:C_OHSK + r * 128 + lo + 64] = 1.0
    mcol = np.full((128, 1), float(m), dtype=f)
    return cst, mcol


_NC_CACHE = {}


def kernel(x, w_in, b_in, conv_w, conv_b, conv_ln_g, conv_ln_b, sinks, w_out, b_out, ln_g, ln_b):
    L, NT = 4, 5
    x = np.asarray(x, dtype=np.float32)
    B, SEQ, _ = x.shape
    params = prep_params(L, w_in, b_in, conv_w, conv_b, conv_ln_g, conv_ln_b, sinks, w_out,
                         b_out, ln_g, ln_b)
    key = (L, NT)
    if key not in _NC_CACHE:
        _NC_CACHE[key] = build_nc(L, NT)
    nc = _NC_CACHE[key]
    in_maps = []
    for c in range(8):
        b, half = c // 2, c % 2
        t0 = half * 2048
        xin = np.zeros((NT * TT, D), dtype=np.float32)
        if half == 0:
            xin[TT:] = x[b, 0:2048]
        else:
            xin[:] = x[b, t0 - TT:t0 + 2048]
        cst, mcol = prep_consts(half)
        mp = dict(params)
        mp.update(x_in=xin, cst_in=cst, m_in=mcol)
        in_maps.append(mp)
    res = run_bass_kernel_spmd(nc, in_maps, core_ids=list(range(8)))
    out = np.empty((B, SEQ, D), dtype=np.float32)
    for c in range(8):
        b, half = c // 2, c % 2
        out[b, half * 2048:(half + 1) * 2048] = res.results[c]["y"]
    return out
```

```python
from contextlib import ExitStack

import numpy as np
import concourse.bass as bass
import concourse.mybir as mybir
from concourse.bass_utils import run_bass_kernel_spmd

F32 = mybir.dt.float32
BF16 = mybir.dt.bfloat16
AF = mybir.ActivationFunctionType
ALU = mybir.AluOpType

D = 2048
CC = 1024
TT = 512
NBLK = 4
KC = 16
CW = 31
PRE = 30
HB_W = 544
ALPHA = float(8 ** 0.25)
EPS = 1e-5
NG_IN = 11
NG = 15
NPP = 44 + 8 * CW + 24
PP_CW = 44
PP_CB = 44 + 8 * CW
PP_LG = PP_CB + 8
PP_LB = PP_LG + 8
C_MA, C_MB, C_MB0, C_ID, C_ONES, C_OHBO, C_OHSK = 0, 128, 256, 384, 512, 640, 1152
NCST = 1664
EPOCH = 8000
LOOKBACK = 6
NWR = 2
NTF = 8
NTB = 4
NPT = 4
NXBF = 2
NDG = 3

def _in_chunks():
    ch = []
    for g in range(4):
        for c in (2 * g, 2 * g + 1):
            ch.append(("val", c))
            ch.append(("glu", c))
    ch += [("kd", 0), ("kd", 1), ("v", 0), ("pad", 0)]
    ch += [("q", i) for i in range(8)]
    ch += [("gA", i) for i in range(8)]
    ch += [("gB", i) for i in range(8)]
    return ch


IN_CHUNKS = _in_chunks()


def _chunk_cols(kind, i):
    if kind == "val":
        return np.arange(i * 128, (i + 1) * 128)
    if kind == "glu":
        return 1024 + np.arange(i * 128, (i + 1) * 128)
    if kind == "gA":
        return 2048 + np.arange(i * 128, (i + 1) * 128)
    if kind == "q":
        return 3072 + np.arange(i * 128, (i + 1) * 128)
    if kind == "kd":
        base = 4096 + i * 64 + np.arange(64)
        return np.concatenate([base, base])
    if kind == "v":
        return 4224 + np.arange(128)
    if kind == "gB":
        return 4352 + np.arange(i * 128, (i + 1) * 128)
    return np.zeros(128, dtype=np.int64)


class Op:
    __slots__ = ("eng", "fn", "deps", "idx", "dma", "tick", "needs_inc", "dval")


class Sched:
    ENGS = ("pe", "act", "dve", "pool", "sp")

    def __init__(self):
        self.ops = {e: [] for e in self.ENGS}
        self.lw = {}
        self.rd = {}
        self.dma_cnt = {}
        self.rr = {}

    def alloc(self, name, n):
        i = self.rr.get(name, 0)
        self.rr[name] = (i + 1) % n
        return i

    def add(self, eng, fn, reads=(), writes=(), dma=None):
        op = Op()
        op.eng = eng
        op.fn = fn
        op.dma = dma
        op.tick = 0
        op.needs_inc = False
        op.dval = 0
        deps = []
        for k in reads:
            w = self.lw.get(k)
            if w is not None:
                deps.append(w)
        for k in writes:
            w = self.lw.get(k)
            if w is not None:
                deps.append(w)
            deps.extend(self.rd.get(k, ()))
        seen = set()
        ud = []
        for d in deps:
            if d is op or id(d) in seen:
                continue
            seen.add(id(d))
            ud.append(d)
        op.deps = ud
        for k in reads:
            self.rd.setdefault(k, []).append(op)
        for k in writes:
            self.lw[k] = op
            self.rd[k] = []
        if dma is not None:
            c = self.dma_cnt.get(dma, 0) + 16
            self.dma_cnt[dma] = c
            op.dval = c
        op.idx = len(self.ops[eng])
        self.ops[eng].append(op)
        return op

    def finalize(self):
        for e in self.ENGS:
            for op in self.ops[e]:
                for d in op.deps:
                    if d.dma is None:
                        d.needs_inc = True
        self.nticks = {}
        for e in self.ENGS:
            t = 0
            for op in self.ops[e]:
                if op.dma is None and op.needs_inc:
                    t += 1
                    op.tick = t
            self.nticks[e] = t

    def emit(self, eng, e, eng_sems, dma_sems):
        seen = {}
        for op in self.ops[eng]:
            for d in op.deps:
                if d.dma is not None:
                    key = ("dma", d.dma)
                    if seen.get(key, 0) >= d.dval:
                        continue
                    seen[key] = d.dval
                    e.wait_ge(dma_sems[d.dma], d.dval)
                else:
                    if d.eng == eng and (eng == "pe" or d.idx < op.idx - LOOKBACK):
                        continue
                    if seen.get(d.eng, 0) >= d.tick:
                        continue
                    seen[d.eng] = d.tick
                    s, v = divmod(d.tick - 1, EPOCH)
                    e.wait_ge(eng_sems[d.eng][s], v + 1)
            ins = op.fn(e)
            if ins is None:
                continue
            if op.dma is not None:
                ins.then_inc(dma_sems[op.dma], 16)
            elif op.needs_inc:
                s, v = divmod(op.tick - 1, EPOCH)
                ins.then_inc(eng_sems[eng][s], 1)


def build_nc(L, NT, main_from=1, pool_conv=(), mask_eng="pool", gate_eng="pool"):
    nc = bass.Bass("TRN2", target_bir_lowering=False)
    NOUT = NT - main_from
    x_in = nc.dram_tensor("x_in", [NT * TT, D], F32, kind="ExternalInput").ap()
    wall = nc.dram_tensor("wall", [L * NG, 128, 8192], F32, kind="ExternalInput").ap()
    pp_in = nc.dram_tensor("pp_in", [128, L * NPP], F32, kind="ExternalInput").ap()
    cst_in = nc.dram_tensor("cst_in", [128, NCST], F32, kind="ExternalInput").ap()
    sk_in = nc.dram_tensor("sk_in", [4, L * 4], F32, kind="ExternalInput").ap()
    bo_in = nc.dram_tensor("bo_in", [4, L * 512], F32, kind="ExternalInput").ap()
    bv_in = nc.dram_tensor("bv_in", [1, L * 128], F32, kind="ExternalInput").ap()
    lng_in = nc.dram_tensor("lng_in", [L, D], F32, kind="ExternalInput").ap()
    lnb_in = nc.dram_tensor("lnb_in", [L, D], F32, kind="ExternalInput").ap()
    m_in = nc.dram_tensor("m_in", [128, 1], F32, kind="ExternalInput").ap()
    y_out = nc.dram_tensor("y", [NOUT * TT, D], F32, kind="ExternalOutput").ap()
    wbf = nc.dram_tensor("wbf", [L * NG, 128, 8192], BF16).ap()

    S = Sched()
    es = ExitStack()

    def sb(name, shape, dt):
        return es.enter_context(nc.sbuf_tensor(name, shape, dt))

    x_tok = sb("x_tok", [128, NBLK, D], F32)
    xT = sb("xT", [128, KC, TT], BF16)
    hb = sb("hb", [128, 8, HB_W], BF16)
    gA = sb("gA", [128, 8, TT], BF16)
    gB = sb("gB", [128, 8, TT], BF16)
    qT = sb("qT", [128, 8, TT], BF16)
    kT = sb("kT", [128, 2, 640], BF16)
    vaug = sb("vaug", [128, 5, 4, 128], BF16)
    U = sb("U", [128, 8, TT], F32)
    pT = [sb(f"pT{i}", [128, 2, TT], BF16) for i in range(NPT)]
    xbf = [sb(f"xbf{i}", [128, D], BF16) for i in range(NXBF)]
    tf = [sb(f"tf{i}", [128, TT], F32) for i in range(NTF)]
    tb = [sb(f"tb{i}", [128, TT], BF16) for i in range(NTB)]
    wr = [sb(f"wr{i}", [128, KC, TT], BF16) for i in range(NWR)]
    pp = sb("pp", [128, L * NPP], F32)
    cst = sb("cst", [128, NCST], BF16)
    skf = sb("skf", [4, L * 4], F32)
    ske = sb("ske", [4, L * 4], F32)
    sk4 = sb("sk4", [4, L * 4, 128], BF16)
    bo4 = sb("bo4", [4, L * 512], BF16)
    bvbc = sb("bvbc", [128, L * 128], F32)
    mcol = sb("mcol", [128, 1], F32)
    hst = sb("hst", [128, L, 8, PRE], BF16)
    kst = sb("kst", [128, L, 2, 128], BF16)
    vst = sb("vst", [128, L, 4, 128], BF16)
    sm = sb("sm", [128, NBLK, 32], F32)
    dgb = [sb(f"dgb{i}", [128, 8, 128], BF16) for i in range(NDG)]
    ps = [es.enter_context(nc.psum_tensor(f"ps{i}", [128, 512], F32)) for i in range(8)]

    def ppc(l, off):
        return pp[:, l * NPP + off:l * NPP + off + 1]

    def cs(off, n=128, rows=None):
        if rows is None:
            return cst[:, off:off + n]
        return cst[0:rows, off:off + n]

    def psb():
        return S.alloc("ps", 8)

    S.add("sp", lambda e: e.dma_start(out=pp[:, :], in_=pp_in), writes=[("pp",)], dma=("su", 0))
    S.add("sp", lambda e: e.dma_start(out=skf[:, :], in_=sk_in), writes=[("skf",)], dma=("su", 1))
    S.add("sp", lambda e: e.dma_start(out=mcol[:, :], in_=m_in), writes=[("mcol",)], dma=("su", 2))
    S.add("sp", lambda e: e.dma_start(out=bvbc[:, :], in_=bv_in.broadcast_to([128, L * 128])),
          writes=[("bvbc",)], dma=("su", 3))
    S.add("pool", lambda e: e.dma_start(out=cst[:, :], in_=cst_in), writes=[("cst",)], dma=("su", 4))
    S.add("pool", lambda e: e.dma_start(out=bo4[:, :], in_=bo_in), writes=[("bo4",)], dma=("su", 5))
    cast_ops = {}
    for l in range(L):
        for g in range(NG):
            i = l * NG + g
            grp = ("wc", i) if l == 0 else ("wcl", l)
            cast_ops[(l, g)] = S.add(
                "pool", (lambda e, i=i: e.dma_start(out=wbf[i], in_=wall[i])),
                writes=[("wbf", l, g)], dma=grp)
        if l > 0:
            for g in range(NG):
                cast_ops[(l, g)].dval = 16 * NG
    S.add("pool", lambda e: e.memset(vaug[:, :, :, :], 1.0), writes=[("v", b) for b in range(5)])
    S.add("pool", lambda e: e.memset(vst[:, :, :, :], 1.0), writes=[("vst", l) for l in range(L)])
    S.add("pool", lambda e: e.memset(kst[:, :, :, :], 0.0), writes=[("kst", l) for l in range(L)])
    S.add("pool", lambda e: e.memset(hst[:, :, :, :], 0.0), writes=[("hst", l) for l in range(L)])
    S.add("pool", lambda e: e.memset(hb[:, :, :], 0.0), writes=[("h", c) for c in range(8)])
    S.add("act", lambda e: e.activation(out=ske[:, :], in_=skf[:, :], func=AF.Exp),
          reads=[("skf",)], writes=[("ske",)])
    S.add("dve", lambda e: e.tensor_copy(
        out=sk4[:, :, :], in_=ske[:, :].unsqueeze(2).broadcast_to([4, L * 4, 128])),
          reads=[("ske",)], writes=[("sk4",)])

    def load_w(l, g):
        slot = S.alloc("wr", NWR)
        i = l * NG + g
        S.add("sp", (lambda e, i=i, slot=slot: e.dma_start(
            out=wr[slot][:, :, :], in_=wbf[i].rearrange("p (k e) -> p k e", k=KC))),
              reads=[("wbf", l, g)], writes=[("w", slot)], dma=("wr", slot))
        return slot

    def prep(b):
        xi = S.alloc("xbf", NXBF)
        S.add("act", (lambda e, b=b, xi=xi: e.activation(out=xbf[xi][:, :], in_=x_tok[:, b, :], func=AF.Copy)),
              reads=[("x", b)], writes=[("xbf", xi)])
        for hlf in range(2):
            bank = psb()
            pv = ps[bank].bitcast(BF16)

            def tfn(e, hlf=hlf, xi=xi, pv=pv):
                ins = None
                for j in range(8):
                    kc = hlf * 8 + j
                    ins = e.transpose(out=pv[:, j * 128:(j + 1) * 128],
                                      in_=xbf[xi][:, kc * 128:(kc + 1) * 128],
                                      identity=cs(C_ID))
                return ins
            S.add("pe", tfn, reads=[("xbf", xi), ("cst",)], writes=[("ps", bank)])
            eng = "act" if hlf == 0 else "dve"

            def efn(e, hlf=hlf, b=b, pv=pv, eng=eng):
                o = xT[:, hlf * 8:(hlf + 1) * 8, b * 128:(b + 1) * 128]
                i_ = pv[:, 0:1024].rearrange("p (a q) -> p a q", a=8)
                if eng == "act":
                    return e.activation(out=o, in_=i_, func=AF.Copy)
                return e.tensor_copy(out=o, in_=i_)
            S.add(eng, efn, reads=[("ps", bank)], writes=[("xT", b, hlf)])

    XT_KEYS = [("xT", b, h) for b in range(NBLK) for h in range(2)]

    def conv_chunk(l, c, eng):
        for k in range(CW):
            def fn(e, k=k, c=c, l=l):
                src = hb[:, c, k:k + TT]
                wk = ppc(l, PP_CW + c * CW + k)
                if k == 0:
                    return e.tensor_scalar(out=U[:, c, :], in0=src, scalar1=wk,
                                           scalar2=ppc(l, PP_CB + c), op0=ALU.mult, op1=ALU.add)
                if eng == "dve":
                    return e.scalar_tensor_tensor(out=U[:, c, :], in0=src, scalar=wk,
                                                  in1=U[:, c, :], op0=ALU.mult, op1=ALU.add)
                return None
            if eng == "dve" or k == 0:
                S.add(eng, fn, reads=[("h", c), ("pp",), ("U", c)] if k else [("h", c), ("pp",)],
                      writes=[("U", c)])
            else:
                ti = S.alloc("tf", NTF)
                S.add("pool", (lambda e, k=k, c=c, l=l, ti=ti: e.tensor_scalar(
                    out=tf[ti][:, :], in0=hb[:, c, k:k + TT], scalar1=ppc(l, PP_CW + c * CW + k),
                    scalar2=None, op0=ALU.mult)),
                      reads=[("h", c), ("pp",)], writes=[("tf", ti)])
                S.add("pool", (lambda e, c=c, ti=ti: e.tensor_tensor(
                    out=U[:, c, :], in0=U[:, c, :], in1=tf[ti][:, :], op=ALU.add)),
                      reads=[("tf", ti), ("U", c)], writes=[("U", c)])

    def conv_pe(l, c):
        bank = psb()
        for g0 in range(0, CW, 8):
            n = min(8, CW - g0)
            di = S.alloc("dg", NDG)

            def dfn(e, g0=g0, n=n, di=di):
                ins = None
                for j in range(n):
                    ins = e.activation(out=dgb[di][:, j, :], in_=cs(C_ID), func=AF.Identity,
                                       scale=ppc(l, PP_CW + c * CW + g0 + j))
                return ins
            S.add("act", dfn, reads=[("cst",), ("pp",)], writes=[("dg", di)])

            def mfn(e, g0=g0, n=n, di=di):
                ins = None
                for j in range(n):
                    k = g0 + j
                    ins = e.matmul(ps[bank][:, :], lhsT=dgb[di][:, j, :], rhs=hb[:, c, k:k + TT],
                                   start=(k == 0), stop=(k == CW - 1))
                return ins
            S.add("pe", mfn, reads=[("dg", di), ("h", c)], writes=[("ps", bank)])
        S.add("act", lambda e: e.activation(out=U[:, c, :], in_=ps[bank][:, :], func=AF.Identity,
                                            bias=ppc(l, PP_CB + c)),
              reads=[("ps", bank), ("pp",)], writes=[("U", c)])

    def tile_layer(t, l, full):
        S.add("pool", lambda e: e.tensor_copy(out=hb[:, :, 0:PRE], in_=hst[:, l, :, :]),
              reads=[("hst", l)], writes=[("h", c) for c in range(8)])
        S.add("pool", lambda e: e.tensor_copy(out=kT[:, :, 0:128], in_=kst[:, l, :, :]),
              reads=[("kst", l)], writes=[("k", 0), ("k", 1)])
        S.add("pool", lambda e: e.tensor_copy(out=vaug[:, 0, :, :], in_=vst[:, l, :, :]),
              reads=[("vst", l)], writes=[("v", 0)])

        def proj_fn(bank, slot, j):
            def fn(e):
                ins = None
                for kc in range(KC):
                    ins = e.matmul(ps[bank][:, :], lhsT=wr[slot][:, kc, j * 128:(j + 1) * 128],
                                   rhs=xT[:, kc, :], start=(kc == 0), stop=(kc == KC - 1))
                return ins
            return fn

        ngroups = NG_IN if full else 5
        pending = []
        for g in range(ngroups):
            ready = list(pending)
            del pending[:]
            slot = load_w(l, g)
            for j in range(4):
                ci = g * 4 + j
                kind, idx = IN_CHUNKS[ci]
                bcol = ppc(l, ci)
                if kind == "pad":
                    continue
                if kind == "v":
                    bank = psb()

                    def vfn(e, bank=bank, slot=slot, j=j):
                        ins = None
                        for b in range(NBLK):
                            for kc in range(KC):
                                ins = e.matmul(ps[bank][:, b * 128:(b + 1) * 128],
                                               lhsT=xT[:, kc, b * 128:(b + 1) * 128],
                                               rhs=wr[slot][:, kc, j * 128:(j + 1) * 128],
                                               start=(kc == 0), stop=(kc == KC - 1))
                        return ins
                    S.add("pe", vfn, reads=[("w", slot)] + XT_KEYS, writes=[("ps", bank)])
                    for b in range(NBLK):
                        for par in range(2):
                            def vev(e, b=b, par=par, bank=bank):
                                o = vaug[:, 1 + b, par:4:2, par * 64:par * 64 + 64]
                                i0 = ps[bank][:, b * 128:(b + 1) * 128].rearrange("p (u d) -> p u d", u=2)
                                i1 = bvbc[:, l * 128:(l + 1) * 128].rearrange("p (u d) -> p u d", u=2)
                                return e.tensor_tensor(out=o, in0=i0, in1=i1, op=ALU.add)
                            S.add("dve", vev, reads=[("ps", bank), ("bvbc",)], writes=[("v", 1 + b)])
                    continue
                if kind == "glu":
                    continue
                bank = psb()
                S.add("pe", proj_fn(bank, slot, j), reads=[("w", slot)] + XT_KEYS,
                      writes=[("ps", bank)])
                if kind == "val":
                    bank_g = psb()
                    S.add("pe", proj_fn(bank_g, slot, j + 1), reads=[("w", slot)] + XT_KEYS,
                          writes=[("ps", bank_g)])
                    ti = S.alloc("tf", NTF)
                    bg = ppc(l, ci + 1)
                    S.add("act", (lambda e, bank_g=bank_g, ti=ti, bg=bg: e.activation(
                        out=tf[ti][:, :], in_=ps[bank_g][:, :], func=AF.Sigmoid, bias=bg)),
                          reads=[("ps", bank_g), ("pp",)], writes=[("tf", ti)])
                    S.add("dve", (lambda e, bank=bank, ti=ti, idx=idx, bcol=bcol: e.scalar_tensor_tensor(
                        out=hb[:, idx, PRE:PRE + TT], in0=ps[bank][:, :], scalar=bcol,
                        in1=tf[ti][:, :], op0=ALU.add, op1=ALU.mult)),
                          reads=[("ps", bank), ("tf", ti), ("pp",)], writes=[("h", idx)])
                    if full:
                        pending.append(idx)
                    continue
                if kind == "kd":
                    S.add("act", (lambda e, bank=bank, idx=idx, bcol=bcol: e.activation(
                        out=kT[:, idx, 128:640], in_=ps[bank][:, :], func=AF.Identity, bias=bcol)),
                          reads=[("ps", bank), ("pp",)], writes=[("k", idx)])
                elif kind == "q":
                    S.add("act", (lambda e, bank=bank, idx=idx, bcol=bcol: e.activation(
                        out=qT[:, idx, :], in_=ps[bank][:, :], func=AF.Identity, bias=bcol)),
                          reads=[("ps", bank), ("pp",)], writes=[("q", idx)])
                elif kind in ("gA", "gB"):
                    dst = gA if kind == "gA" else gB
                    S.add("act", (lambda e, bank=bank, idx=idx, bcol=bcol, dst=dst: e.activation(
                        out=dst[:, idx, :], in_=ps[bank][:, :], func=AF.Silu, bias=bcol)),
                          reads=[("ps", bank), ("pp",)], writes=[(kind, idx)])

            for c_ in ready:
                conv_pe(l, c_)
        for c_ in pending:
            conv_pe(l, c_)

        if t == 0:
            S.add("pool", lambda e: e.tensor_scalar(
                out=hst[:, l, :, :], in0=hb[:, :, TT:TT + PRE], scalar1=mcol[:, 0:1], scalar2=None,
                op0=ALU.mult),
                  reads=[("h", c) for c in range(8)] + [("mcol",)], writes=[("hst", l)])
        else:
            S.add("pool", lambda e: e.tensor_copy(out=hst[:, l, :, :], in_=hb[:, :, TT:TT + PRE]),
                  reads=[("h", c) for c in range(8)], writes=[("hst", l)])
        S.add("pool", lambda e: e.tensor_copy(out=kst[:, l, :, :], in_=kT[:, :, TT:TT + 128]),
              reads=[("k", 0), ("k", 1)], writes=[("kst", l)])
        S.add("pool", lambda e: e.tensor_copy(out=vst[:, l, :, :], in_=vaug[:, 4, :, :]),
              reads=[("v", 4)], writes=[("vst", l)])
        if not full:
            return

        iters = [(qb, u, par) for qb in range(NBLK) for u in range(2) for par in range(2)]
        st = {}

        def att_front(i):
            qb, u, par = iters[i]
            pr = slice(par * 64, par * 64 + 64)
            pslot = S.alloc("pT", NPT)
            banks = []
            for kb in range(2):
                bank = psb()
                banks.append(bank)
                k0 = (qb + kb) * 128

                def sfn(e, bank=bank, k0=k0, u=u, pr=pr, qb=qb):
                    return e.matmul(
                        ps[bank][:, :].rearrange("p (a q) -> p a q", a=4),
                        lhsT=kT[pr, u, k0:k0 + 128],
                        rhs=qT[pr, 4 * u:4 * u + 4, qb * 128:(qb + 1) * 128],
                        start=True, stop=True)
                S.add("pe", sfn, reads=[("k", u)] + [("q", 4 * u + a) for a in range(4)],
                      writes=[("ps", bank)])
            for kb in range(2):
                S.add("act", (lambda e, bank=banks[kb], pslot=pslot, kb=kb: e.activation(
                    out=pT[pslot][:, kb, :], in_=ps[bank][:, :], func=AF.Exp, scale=0.125)),
                      reads=[("ps", banks[kb])], writes=[("pT", pslot, kb)])
                if kb == 0:
                    moff = C_MB0 if (t == main_from and qb == 0 and main_from > 0) else C_MB
                else:
                    moff = C_MA

                def mfn(e, pslot=pslot, kb=kb, moff=moff):
                    v_ = pT[pslot][:, kb, :].rearrange("p (a q) -> p a q", a=4)
                    return e.tensor_tensor(
                        out=v_, in0=v_,
                        in1=cs(moff).unsqueeze(1).broadcast_to([128, 4, 128]),
                        op=ALU.mult)
                S.add(mask_eng, mfn, reads=[("pT", pslot, kb), ("cst",)],
                      writes=[("pT", pslot, kb)])
            st[i] = pslot

        def att_back(i):
            qb, u, par = iters[i]
            pslot = st[i]
            orow = slice(par * 64, par * 64 + 64)
            drow = slice((1 - par) * 64, (1 - par) * 64 + 64)
            r = par * 2 + u
            var = u * 2 + par
            bank = psb()

            def pvfn(e):
                e.matmul(ps[bank][:, :], lhsT=vaug[:, qb, var, :], rhs=pT[pslot][:, 0, :],
                         start=True, stop=False)
                e.matmul(ps[bank][:, :], lhsT=vaug[:, qb + 1, var, :], rhs=pT[pslot][:, 1, :],
                         start=False, stop=False)
                return e.matmul(ps[bank][:, :], lhsT=cs(C_OHSK + r * 128, 128, rows=4),
                                rhs=sk4[:, l * 4:(l + 1) * 4, :], start=False, stop=True)
            S.add("pe", pvfn, reads=[("v", qb), ("v", qb + 1), ("pT", pslot, 0),
                                     ("pT", pslot, 1), ("cst",), ("sk4",)],
                  writes=[("ps", bank)])
            t1 = S.alloc("tf", NTF)
            S.add("dve", lambda e: e.reciprocal(out=tf[t1][orow, :], in_=ps[bank][drow, :]),
                  reads=[("ps", bank)], writes=[("tf", t1)])
            t2 = S.alloc("tf", NTF)
            S.add("dve", lambda e: e.tensor_tensor(
                out=tf[t2][orow, :], in0=ps[bank][orow, :], in1=tf[t1][orow, :], op=ALU.mult),
                  reads=[("ps", bank), ("tf", t1)], writes=[("tf", t2)])

            def gfn(e):
                o = gB[orow, 4 * u:4 * u + 4, qb * 128:(qb + 1) * 128]
                return e.tensor_tensor(
                    out=o, in0=o, in1=tf[t2][orow, :].rearrange("p (a q) -> p a q", a=4),
                    op=ALU.mult)
            S.add(gate_eng, gfn, reads=[("tf", t2)] + [("gB", 4 * u + a) for a in range(4)],
                  writes=[("gB", 4 * u + a) for a in range(4)])

        DEPTH_ATT = 2
        for i in range(min(DEPTH_ATT, len(iters))):
            att_front(i)
        for i in range(len(iters)):
            att_back(i)
            if i + DEPTH_ATT < len(iters):
                att_front(i + DEPTH_ATT)

        b1 = psb()
        b2 = psb()
        for c in range(8):
            i1 = S.alloc("tb", NTB)
            S.add("act", (lambda e, c=c, i1=i1: e.activation(out=tb[i1][:, :], in_=U[:, c, :], func=AF.Copy)),
                  reads=[("U", c)], writes=[("tb", i1)])
            i2 = S.alloc("tb", NTB)
            S.add("act", (lambda e, c=c, i2=i2: e.activation(out=tb[i2][:, :], in_=U[:, c, :], func=AF.Square)),
                  reads=[("U", c)], writes=[("tb", i2)])

            def stf(e, c=c, i1=i1, i2=i2):
                e.matmul(ps[b1][:, :], lhsT=cs(C_ONES), rhs=tb[i1][:, :], start=(c == 0), stop=(c == 7))
                return e.matmul(ps[b2][:, :], lhsT=cs(C_ONES), rhs=tb[i2][:, :], start=(c == 0), stop=(c == 7))
            S.add("pe", stf, reads=[("tb", i1), ("tb", i2), ("cst",)], writes=[("ps", b1), ("ps", b2)])
        imu = S.alloc("tf", NTF)
        S.add("dve", lambda e: e.tensor_scalar(out=tf[imu][:, :], in0=ps[b1][:, :], scalar1=1.0 / CC,
                                               scalar2=None, op0=ALU.mult),
              reads=[("ps", b1)], writes=[("tf", imu)])
        imq = S.alloc("tf", NTF)
        S.add("dve", lambda e: e.tensor_tensor(out=tf[imq][:, :], in0=tf[imu][:, :], in1=tf[imu][:, :],
                                               op=ALU.mult),
              reads=[("tf", imu)], writes=[("tf", imq)])
        ivar = S.alloc("tf", NTF)
        S.add("dve", lambda e: e.scalar_tensor_tensor(out=tf[ivar][:, :], in0=ps[b2][:, :], scalar=1.0 / CC,
                                                      in1=tf[imq][:, :], op0=ALU.mult, op1=ALU.subtract),
              reads=[("ps", b2), ("tf", imq)], writes=[("tf", ivar)])
        S.add("act", lambda e: e.activation(out=tf[imq][:, :], in_=tf[ivar][:, :], func=AF.Sqrt, bias=EPS),
              reads=[("tf", ivar)], writes=[("tf", imq)])
        S.add("dve", lambda e: e.reciprocal(out=tf[ivar][:, :], in_=tf[imq][:, :]),
              reads=[("tf", imq)], writes=[("tf", ivar)])
        irs = ivar
        for c in range(8):
            S.add("dve", (lambda e, c=c: e.tensor_tensor(out=U[:, c, :], in0=U[:, c, :], in1=tf[imu][:, :],
                                                         op=ALU.subtract)),
                  reads=[("U", c), ("tf", imu)], writes=[("U", c)])
            S.add("dve", (lambda e, c=c: e.tensor_tensor(out=U[:, c, :], in0=U[:, c, :], in1=tf[irs][:, :],
                                                         op=ALU.mult)),
                  reads=[("U", c), ("tf", irs)], writes=[("U", c)])
            i1 = S.alloc("tb", NTB)
            S.add("act", (lambda e, c=c, i1=i1: e.activation(
                out=tb[i1][:, :], in_=U[:, c, :], func=AF.Silu, scale=ppc(l, PP_LG + c),
                bias=ppc(l, PP_LB + c))),
                  reads=[("U", c), ("pp",)], writes=[("tb", i1)])
            S.add(gate_eng, (lambda e, c=c, i1=i1: e.tensor_tensor(
                out=gA[:, c, :], in0=gA[:, c, :], in1=tb[i1][:, :], op=ALU.mult)),
                  reads=[("tb", i1), ("gA", c)], writes=[("gA", c)])
        S.add("sp", lambda e: e.dma_start(out=U[:, 0:4, :].rearrange("p a q -> p (a q)"),
                                          in_=lng_in[l:l + 1, :].broadcast_to([128, D])),
              writes=[("U", c) for c in range(4)], dma=("lg", 0))
        S.add("sp", lambda e: e.dma_start(out=U[:, 4:8, :].rearrange("p a q -> p (a q)"),
                                          in_=lnb_in[l:l + 1, :].broadcast_to([128, D])),
              writes=[("U", c) for c in range(4, 8)], dma=("lg", 1))

        for dg in range(4):
            slot = load_w(l, NG_IN + dg)
            for b in range(NBLK):
                bank = psb()

                def ofn(e, bank=bank, slot=slot, b=b, dg=dg):
                    for ec in range(16):
                        ysrc = gA if ec < 8 else gB
                        e.matmul(ps[bank][:, :], lhsT=ysrc[:, ec % 8, b * 128:(b + 1) * 128],
                                 rhs=wr[slot][:, ec, :], start=(ec == 0), stop=False)
                    return e.matmul(ps[bank][:, :], lhsT=cs(C_OHBO + dg * 128, 128, rows=4),
                                    rhs=bo4[:, l * 512:(l + 1) * 512], start=False, stop=True)
                S.add("pe", ofn, reads=[("w", slot), ("cst",), ("bo4",)] +
                      [("gA", c) for c in range(8)] + [("gB", c) for c in range(8)],
                      writes=[("ps", bank)])
                S.add("dve", (lambda e, bank=bank, b=b, dg=dg: e.scalar_tensor_tensor(
                    out=x_tok[:, b, dg * 512:(dg + 1) * 512], in0=x_tok[:, b, dg * 512:(dg + 1) * 512],
                    scalar=ALPHA, in1=ps[bank][:, :], op0=ALU.mult, op1=ALU.add)),
                      reads=[("ps", bank), ("x", b)], writes=[("x", b)])
        last = (l == L - 1)
        for b in range(NBLK):
            for dg in range(4):
                S.add("dve", (lambda e, b=b, dg=dg: e.bn_stats(
                    out=sm[:, b, dg * 6:(dg + 1) * 6], in_=x_tok[:, b, dg * 512:(dg + 1) * 512])),
                      reads=[("x", b)], writes=[("sm", b, dg)])
            S.add("dve", (lambda e, b=b: e.bn_aggr(out=sm[:, b, 24:26], in_=sm[:, b, 0:24])),
                  reads=[("sm", b, dg) for dg in range(4)], writes=[("sm", b, 4)])
            S.add("act", (lambda e, b=b: e.activation(out=sm[:, b, 26:27], in_=sm[:, b, 25:26],
                                                      func=AF.Sqrt, bias=EPS)),
                  reads=[("sm", b, 4)], writes=[("sm", b, 5)])
            S.add("dve", (lambda e, b=b: e.reciprocal(out=sm[:, b, 27:28], in_=sm[:, b, 26:27])),
                  reads=[("sm", b, 5)], writes=[("sm", b, 6)])
            S.add("dve", (lambda e, b=b: e.tensor_scalar(
                out=sm[:, b, 28:29], in0=sm[:, b, 24:25], scalar1=-1.0, scalar2=sm[:, b, 27:28],
                op0=ALU.mult, op1=ALU.mult)),
                  reads=[("sm", b, 4), ("sm", b, 6)], writes=[("sm", b, 7)])
            for hf in range(2):
                cs_ = slice(hf * 1024, (hf + 1) * 1024)
                S.add("dve", (lambda e, b=b, cs_=cs_, hf=hf: e.scalar_tensor_tensor(
                    out=x_tok[:, b, cs_], in0=x_tok[:, b, cs_], scalar=sm[:, b, 24:25],
                    in1=U[:, 2 * hf:2 * hf + 2, :].rearrange("p a q -> p (a q)"),
                    op0=ALU.subtract, op1=ALU.mult)),
                      reads=[("x", b), ("sm", b, 4)] + [("U", c) for c in range(4)], writes=[("x", b)])
                S.add("dve", (lambda e, b=b, cs_=cs_, hf=hf: e.scalar_tensor_tensor(
                    out=x_tok[:, b, cs_], in0=x_tok[:, b, cs_], scalar=sm[:, b, 27:28],
                    in1=U[:, 4 + 2 * hf:6 + 2 * hf, :].rearrange("p a q -> p (a q)"),
                    op0=ALU.mult, op1=ALU.add)),
                      reads=[("x", b), ("sm", b, 6)] + [("U", c) for c in range(4, 8)], writes=[("x", b)])
            if last:
                row = ((t - main_from) * NBLK + b) * 128
                out_ops.append(S.add("sp", (lambda e, b=b, row=row: e.dma_start(
                    out=y_out[row:row + 128, :], in_=x_tok[:, b, :])),
                    reads=[("x", b)], dma=("xo", b)))
            else:
                prep(b)

    out_ops = []
    for t in range(NT):
        for b in range(NBLK):
            row = (t * NBLK + b) * 128
            S.add("sp", (lambda e, b=b, row=row: e.dma_start(out=x_tok[:, b, :], in_=x_in[row:row + 128, :])),
                  writes=[("x", b)], dma=("xi", b))
            prep(b)
        for l in range(L):
            full = not (t < main_from and l == L - 1)
            tile_layer(t, l, full)
    fin = S.add("sp", lambda e: None)
    fin.deps = list(out_ops)

    S.finalize()
    dma_keys = list(S.dma_cnt.keys())
    eng_sems = {}
    for en in S.ENGS:
        n = max(1, -(-S.nticks[en] // EPOCH))
        eng_sems[en] = [es.enter_context(nc.semaphore(f"s_{en}{i}")) for i in range(n)]
    dma_sems = {k: es.enter_context(nc.semaphore("d_" + "_".join(str(x) for x in k))) for k in dma_keys}
    with nc.Block() as block:
        @block.tensor
        def _(e):
            S.emit("pe", e, eng_sems, dma_sems)

        @block.scalar
        def _(e):
            S.emit("act", e, eng_sems, dma_sems)

        @block.vector
        def _(e):
            S.emit("dve", e, eng_sems, dma_sems)

        @block.gpsimd
        def _(e):
            S.emit("pool", e, eng_sems, dma_sems)

        @block.sync
        def _(e):
            S.emit("sp", e, eng_sems, dma_sems)
    es.close()
    return nc


def prep_params(L, w_in, b_in, conv_w, conv_b, conv_ln_g, conv_ln_b, sinks, w_out, b_out,
                ln_g, ln_b):
    f = np.float32
    wall = np.empty((L * NG, 128, 8192), dtype=f)
    pp = np.zeros((128, L * NPP), dtype=f)
    sk = np.zeros((4, L * 4), dtype=f)
    bo = np.zeros((4, L * 512), dtype=f)
    bv = np.zeros((1, L * 128), dtype=f)
    for l in range(L):
        cols = np.concatenate([_chunk_cols(k, i) for (k, i) in IN_CHUNKS])
        wi = np.asarray(w_in[l])[:, cols]
        wi = wi.reshape(KC, 128, NG_IN, 512).transpose(2, 1, 0, 3)
        wall[l * NG:l * NG + NG_IN] = wi.reshape(NG_IN, 128, 8192)
        wo = np.asarray(w_out[l]).reshape(KC, 128, 4, 512).transpose(2, 1, 0, 3)
        wall[l * NG + NG_IN:(l + 1) * NG] = wo.reshape(4, 128, 8192)
        bi = np.asarray(b_in[l])[cols].reshape(44, 128).T
        pp[:, l * NPP:l * NPP + 44] = bi
        cw = np.asarray(conv_w[l]).reshape(CW, 8, 128).transpose(2, 1, 0)
        pp[:, l * NPP + PP_CW:l * NPP + PP_CB] = cw.reshape(128, 8 * CW)
        pp[:, l * NPP + PP_CB:l * NPP + PP_LG] = np.asarray(conv_b[l]).reshape(8, 128).T
        pp[:, l * NPP + PP_LG:l * NPP + PP_LB] = np.asarray(conv_ln_g[l]).reshape(8, 128).T
        pp[:, l * NPP + PP_LB:l * NPP + PP_LB + 8] = np.asarray(conv_ln_b[l]).reshape(8, 128).T
        sl = np.asarray(sinks[l])
        for par in range(2):
            for u in range(2):
                for k in range(4):
                    sk[par * 2 + u, l * 4 + k] = sl[8 * u + 2 * k + par]
        bo[:, l * 512:(l + 1) * 512] = np.asarray(b_out[l]).reshape(4, 512)
        bv[0, l * 128:(l + 1) * 128] = np.asarray(b_in[l])[4224:4352]
    return dict(wall=wall, pp_in=pp, sk_in=sk, bo_in=bo, bv_in=bv,
                lng_in=np.ascontiguousarray(np.asarray(ln_g)[:L], dtype=f),
                lnb_in=np.ascontiguousarray(np.asarray(ln_b)[:L], dtype=f))


def prep_consts(m):
    f = np.float32
    cst = np.zeros((128, NCST), dtype=f)
    s = np.arange(128)[:, None]
    q = np.arange(128)[None, :]
    cst[:, C_MA:C_MA + 128] = (q >= s)
    cst[:, C_MB:C_MB + 128] = (q < s)
    cst[:, C_MB0:C_MB0 + 128] = (q < s) * float(m)
    cst[:, C_ID:C_ID + 128] = np.eye(128)
    cst[:, C_ONES:C_ONES + 128] = 1.0
    for dg in range(4):
        cst[dg, C_OHBO + dg * 128:C_OHBO + (dg + 1) * 128] = 1.0
    for par in range(2):
        for u in range(2):
            r = par * 2 + u
            lo = (1 - par) * 64
            cst[r, C_OHSK + r * 128 + lo:C_OHSK + r * 128 + lo + 64] = 1.0
    mcol = np.full((128, 1), float(m), dtype=f)
    return cst, mcol


_NC_CACHE = {}


def kernel(x, w_in, b_in, conv_w, conv_b, conv_ln_g, conv_ln_b, sinks, w_out, b_out, ln_g, ln_b):
    L, NT = 4, 5
    x = np.asarray(x, dtype=np.float32)
    B, SEQ, _ = x.shape
    params = prep_params(L, w_in, b_in, conv_w, conv_b, conv_ln_g, conv_ln_b, sinks, w_out,
                         b_out, ln_g, ln_b)
    key = (L, NT)
    if key not in _NC_CACHE:
        _NC_CACHE[key] = build_nc(L, NT)
    nc = _NC_CACHE[key]
    in_maps = []
    for c in range(8):
        b, half = c // 2, c % 2
        t0 = half * 2048
        xin = np.zeros((NT * TT, D), dtype=np.float32)
        if half == 0:
            xin[TT:] = x[b, 0:2048]
        else:
            xin[:] = x[b, t0 - TT:t0 + 2048]
        cst, mcol = prep_consts(half)
        mp = dict(params)
        mp.update(x_in=xin, cst_in=cst, m_in=mcol)
        in_maps.append(mp)
    res = run_bass_kernel_spmd(nc, in_maps, core_ids=list(range(8)))
    out = np.empty((B, SEQ, D), dtype=np.float32)
    for c in range(8):
        b, half = c // 2, c % 2
        out[b, half * 2048:(half + 1) * 2048] = res.results[c]["y"]
    return out
```

```python
from contextlib import ExitStack

import numpy as np
import concourse.bass as bass
import concourse.mybir as mybir
from concourse.bass_utils import run_bass_kernel_spmd

F32 = mybir.dt.float32
BF16 = mybir.dt.bfloat16
AF = mybir.ActivationFunctionType
ALU = mybir.AluOpType

D = 2048
CC = 1024
TT = 512
NBLK = 4
KC = 16
CW = 31
PRE = 30
HB_W = 544
ALPHA = float(8 ** 0.25)
EPS = 1e-5
NG_IN = 11
NG = 15
NPP = 44 + 8 * CW + 24
PP_CW = 44
PP_CB = 44 + 8 * CW
PP_LG = PP_CB + 8
PP_LB = PP_LG + 8
C_MA, C_MB, C_MB0, C_ID, C_ONES, C_OHBO, C_OHSK = 0, 128, 256, 384, 512, 640, 1152
NCST = 1664
EPOCH = 8000
LOOKBACK = 6
NWR = 2
NTF = 8
NTB = 4
NPT = 4
NXBF = 2
NDG = 3

def _in_chunks():
    ch = []
    for g in range(4):
        for c in (2 * g, 2 * g + 1):
            ch.append(("val", c))
            ch.append(("glu", c))
    ch += [("kd", 0), ("kd", 1), ("v", 0), ("pad", 0)]
    ch += [("q", i) for i in range(8)]
    ch += [("gA", i) for i in range(8)]
    ch += [("gB", i) for i in range(8)]
    return ch


IN_CHUNKS = _in_chunks()


def _chunk_cols(kind, i):
    if kind == "val":
        return np.arange(i * 128, (i + 1) * 128)
    if kind == "glu":
        return 1024 + np.arange(i * 128, (i + 1) * 128)
    if kind == "gA":
        return 2048 + np.arange(i * 128, (i + 1) * 128)
    if kind == "q":
        return 3072 + np.arange(i * 128, (i + 1) * 128)
    if kind == "kd":
        base = 4096 + i * 64 + np.arange(64)
        return np.concatenate([base, base])
    if kind == "v":
        return 4224 + np.arange(128)
    if kind == "gB":
        return 4352 + np.arange(i * 128, (i + 1) * 128)
    return np.zeros(128, dtype=np.int64)


class Op:
    __slots__ = ("eng", "fn", "deps", "idx", "dma", "tick", "needs_inc", "dval")


class Sched:
    ENGS = ("pe", "act", "dve", "pool", "sp")

    def __init__(self):
        self.ops = {e: [] for e in self.ENGS}
        self.lw = {}
        self.rd = {}
        self.dma_cnt = {}
        self.rr = {}

    def alloc(self, name, n):
        i = self.rr.get(name, 0)
        self.rr[name] = (i + 1) % n
        return i

    def add(self, eng, fn, reads=(), writes=(), dma=None):
        op = Op()
        op.eng = eng
        op.fn = fn
        op.dma = dma
        op.tick = 0
        op.needs_inc = False
        op.dval = 0
        deps = []
        for k in reads:
            w = self.lw.get(k)
            if w is not None:
                deps.append(w)
        for k in writes:
            w = self.lw.get(k)
            if w is not None:
                deps.append(w)
            deps.extend(self.rd.get(k, ()))
        seen = set()
        ud = []
        for d in deps:
            if d is op or id(d) in seen:
                continue
            seen.add(id(d))
            ud.append(d)
        op.deps = ud
        for k in reads:
            self.rd.setdefault(k, []).append(op)
        for k in writes:
            self.lw[k] = op
            self.rd[k] = []
        if dma is not None:
            c = self.dma_cnt.get(dma, 0) + 16
            self.dma_cnt[dma] = c
            op.dval = c
        op.idx = len(self.ops[eng])
        self.ops[eng].append(op)
        return op

    def finalize(self):
        for e in self.ENGS:
            for op in self.ops[e]:
                for d in op.deps:
                    if d.dma is None:
                        d.needs_inc = True
        self.nticks = {}
        for e in self.ENGS:
            t = 0
            for op in self.ops[e]:
                if op.dma is None and op.needs_inc:
                    t += 1
                    op.tick = t
            self.nticks[e] = t

    def emit(self, eng, e, eng_sems, dma_sems):
        seen = {}
        for op in self.ops[eng]:
            for d in op.deps:
                if d.dma is not None:
                    key = ("dma", d.dma)
                    if seen.get(key, 0) >= d.dval:
                        continue
                    seen[key] = d.dval
                    e.wait_ge(dma_sems[d.dma], d.dval)
                else:
                    if d.eng == eng and (eng == "pe" or d.idx < op.idx - LOOKBACK):
                        continue
                    if seen.get(d.eng, 0) >= d.tick:
                        continue
                    seen[d.eng] = d.tick
                    s, v = divmod(d.tick - 1, EPOCH)
                    e.wait_ge(eng_sems[d.eng][s], v + 1)
            ins = op.fn(e)
            if ins is None:
                continue
            if op.dma is not None:
                ins.then_inc(dma_sems[op.dma], 16)
            elif op.needs_inc:
                s, v = divmod(op.tick - 1, EPOCH)
                ins.then_inc(eng_sems[eng][s], 1)


def build_nc(L, NT, main_from=1, pool_conv=(), mask_eng="pool", gate_eng="pool"):
    nc = bass.Bass("TRN2", target_bir_lowering=False)
    NOUT = NT - main_from
    x_in = nc.dram_tensor("x_in", [NT * TT, D], F32, kind="ExternalInput").ap()
    wall = nc.dram_tensor("wall", [L * NG, 128, 8192], F32, kind="ExternalInput").ap()
    pp_in = nc.dram_tensor("pp_in", [128, L * NPP], F32, kind="ExternalInput").ap()
    cst_in = nc.dram_tensor("cst_in", [128, NCST], F32, kind="ExternalInput").ap()
    sk_in = nc.dram_tensor("sk_in", [4, L * 4], F32, kind="ExternalInput").ap()
    bo_in = nc.dram_tensor("bo_in", [4, L * 512], F32, kind="ExternalInput").ap()
    bv_in = nc.dram_tensor("bv_in", [1, L * 128], F32, kind="ExternalInput").ap()
    lng_in = nc.dram_tensor("lng_in", [L, D], F32, kind="ExternalInput").ap()
    lnb_in = nc.dram_tensor("lnb_in", [L, D], F32, kind="ExternalInput").ap()
    m_in = nc.dram_tensor("m_in", [128, 1], F32, kind="ExternalInput").ap()
    y_out = nc.dram_tensor("y", [NOUT * TT, D], F32, kind="ExternalOutput").ap()
    wbf = nc.dram_tensor("wbf", [L * NG, 128, 8192], BF16).ap()

    S = Sched()
    es = ExitStack()

    def sb(name, shape, dt):
        return es.enter_context(nc.sbuf_tensor(name, shape, dt))

    x_tok = sb("x_tok", [128, NBLK, D], F32)
    xT = sb("xT", [128, KC, TT], BF16)
    hb = sb("hb", [128, 8, HB_W], BF16)
    gA = sb("gA", [128, 8, TT], BF16)
    gB = sb("gB", [128, 8, TT], BF16)
    qT = sb("qT", [128, 8, TT], BF16)
    kT = sb("kT", [128, 2, 640], BF16)
    vaug = sb("vaug", [128, 5, 4, 128], BF16)
    U = sb("U", [128, 8, TT], F32)
    pT = [sb(f"pT{i}", [128, 2, TT], BF16) for i in range(NPT)]
    xbf = [sb(f"xbf{i}", [128, D], BF16) for i in range(NXBF)]
    tf = [sb(f"tf{i}", [128, TT], F32) for i in range(NTF)]
    tb = [sb(f"tb{i}", [128, TT], BF16) for i in range(NTB)]
    wr = [sb(f"wr{i}", [128, KC, TT], BF16) for i in range(NWR)]
    pp = sb("pp", [128, L * NPP], F32)
    cst = sb("cst", [128, NCST], BF16)
    skf = sb("skf", [4, L * 4], F32)
    ske = sb("ske", [4, L * 4], F32)
    sk4 = sb("sk4", [4, L * 4, 128], BF16)
    bo4 = sb("bo4", [4, L * 512], BF16)
    bvbc = sb("bvbc", [128, L * 128], F32)
    mcol = sb("mcol", [128, 1], F32)
    hst = sb("hst", [128, L, 8, PRE], BF16)
    kst = sb("kst", [128, L, 2, 128], BF16)
    vst = sb("vst", [128, L, 4, 128], BF16)
    sm = sb("sm", [128, NBLK, 32], F32)
    dgb = [sb(f"dgb{i}", [128, 8, 128], BF16) for i in range(NDG)]
    ps = [es.enter_context(nc.psum_tensor(f"ps{i}", [128, 512], F32)) for i in range(8)]

    def ppc(l, off):
        return pp[:, l * NPP + off:l * NPP + off + 1]

    def cs(off, n=128, rows=None):
        if rows is None:
            return cst[:, off:off + n]
        return cst[0:rows, off:off + n]

    def psb():
        return S.alloc("ps", 8)

    S.add("sp", lambda e: e.dma_start(out=pp[:, :], in_=pp_in), writes=[("pp",)], dma=("su", 0))
    S.add("sp", lambda e: e.dma_start(out=skf[:, :], in_=sk_in), writes=[("skf",)], dma=("su", 1))
    S.add("sp", lambda e: e.dma_start(out=mcol[:, :], in_=m_in), writes=[("mcol",)], dma=("su", 2))
    S.add("sp", lambda e: e.dma_start(out=bvbc[:, :], in_=bv_in.broadcast_to([128, L * 128])),
          writes=[("bvbc",)], dma=("su", 3))
    S.add("pool", lambda e: e.dma_start(out=cst[:, :], in_=cst_in), writes=[("cst",)], dma=("su", 4))
    S.add("pool", lambda e: e.dma_start(out=bo4[:, :], in_=bo_in), writes=[("bo4",)], dma=("su", 5))
    cast_ops = {}
    for l in range(L):
        for g in range(NG):
            i = l * NG + g
            grp = ("wc", i) if l == 0 else ("wcl", l)
            cast_ops[(l, g)] = S.add(
                "pool", (lambda e, i=i: e.dma_start(out=wbf[i], in_=wall[i])),
                writes=[("wbf", l, g)], dma=grp)
        if l > 0:
            for g in range(NG):
                cast_ops[(l, g)].dval = 16 * NG
    S.add("pool", lambda e: e.memset(vaug[:, :, :, :], 1.0), writes=[("v", b) for b in range(5)])
    S.add("pool", lambda e: e.memset(vst[:, :, :, :], 1.0), writes=[("vst", l) for l in range(L)])
    S.add("pool", lambda e: e.memset(kst[:, :, :, :], 0.0), writes=[("kst", l) for l in range(L)])
    S.add("pool", lambda e: e.memset(hst[:, :, :, :], 0.0), writes=[("hst", l) for l in range(L)])
    S.add("pool", lambda e: e.memset(hb[:, :, :], 0.0), writes=[("h", c) for c in range(8)])
    S.add("act", lambda e: e.activation(out=ske[:, :], in_=skf[:, :], func=AF.Exp),
          reads=[("skf",)], writes=[("ske",)])
    S.add("dve", lambda e: e.tensor_copy(
        out=sk4[:, :, :], in_=ske[:, :].unsqueeze(2).broadcast_to([4, L * 4, 128])),
          reads=[("ske",)], writes=[("sk4",)])

    def load_w(l, g):
        slot = S.alloc("wr", NWR)
        i = l * NG + g
        S.add("sp", (lambda e, i=i, slot=slot: e.dma_start(
            out=wr[slot][:, :, :], in_=wbf[i].rearrange("p (k e) -> p k e", k=KC))),
              reads=[("wbf", l, g)], writes=[("w", slot)], dma=("wr", slot))
        return slot

    def prep(b):
        xi = S.alloc("xbf", NXBF)
        S.add("act", (lambda e, b=b, xi=xi: e.activation(out=xbf[xi][:, :], in_=x_tok[:, b, :], func=AF.Copy)),
              reads=[("x", b)], writes=[("xbf", xi)])
        for hlf in range(2):
            bank = psb()
            pv = ps[bank].bitcast(BF16)

            def tfn(e, hlf=hlf, xi=xi, pv=pv):
                ins = None
                for j in range(8):
                    kc = hlf * 8 + j
                    ins = e.transpose(out=pv[:, j * 128:(j + 1) * 128],
                                      in_=xbf[xi][:, kc * 128:(kc + 1) * 128],
                                      identity=cs(C_ID))
                return ins
            S.add("pe", tfn, reads=[("xbf", xi), ("cst",)], writes=[("ps", bank)])
            eng = "act" if hlf == 0 else "dve"

            def efn(e, hlf=hlf, b=b, pv=pv, eng=eng):
                o = xT[:, hlf * 8:(hlf + 1) * 8, b * 128:(b + 1) * 128]
                i_ = pv[:, 0:1024].rearrange("p (a q) -> p a q", a=8)
                if eng == "act":
                    return e.activation(out=o, in_=i_, func=AF.Copy)
                return e.tensor_copy(out=o, in_=i_)
            S.add(eng, efn, reads=[("ps", bank)], writes=[("xT", b, hlf)])

    XT_KEYS = [("xT", b, h) for b in range(NBLK) for h in range(2)]

    def conv_chunk(l, c, eng):
        for k in range(CW):
            def fn(e, k=k, c=c, l=l):
                src = hb[:, c, k:k + TT]
                wk = ppc(l, PP_CW + c * CW + k)
                if k == 0:
                    return e.tensor_scalar(out=U[:, c, :], in0=src, scalar1=wk,
                                           scalar2=ppc(l, PP_CB + c), op0=ALU.mult, op1=ALU.add)
                if eng == "dve":
                    return e.scalar_tensor_tensor(out=U[:, c, :], in0=src, scalar=wk,
                                                  in1=U[:, c, :], op0=ALU.mult, op1=ALU.add)
                return None
            if eng == "dve" or k == 0:
                S.add(eng, fn, reads=[("h", c), ("pp",), ("U", c)] if k else [("h", c), ("pp",)],
                      writes=[("U", c)])
            else:
                ti = S.alloc("tf", NTF)
                S.add("pool", (lambda e, k=k, c=c, l=l, ti=ti: e.tensor_scalar(
                    out=tf[ti][:, :], in0=hb[:, c, k:k + TT], scalar1=ppc(l, PP_CW + c * CW + k),
                    scalar2=None, op0=ALU.mult)),
                      reads=[("h", c), ("pp",)], writes=[("tf", ti)])
                S.add("pool", (lambda e, c=c, ti=ti: e.tensor_tensor(
                    out=U[:, c, :], in0=U[:, c, :], in1=tf[ti][:, :], op=ALU.add)),
                      reads=[("tf", ti), ("U", c)], writes=[("U", c)])

    def conv_pe(l, c):
        bank = psb()
        for g0 in range(0, CW, 8):
            n = min(8, CW - g0)
            di = S.alloc("dg", NDG)

            def dfn(e, g0=g0, n=n, di=di):
                ins = None
                for j in range(n):
                    ins = e.activation(out=dgb[di][:, j, :], in_=cs(C_ID), func=AF.Identity,
                                       scale=ppc(l, PP_CW + c * CW + g0 + j))
                return ins
            S.add("act", dfn, reads=[("cst",), ("pp",)], writes=[("dg", di)])

            def mfn(e, g0=g0, n=n, di=di):
                ins = None
                for j in range(n):
                    k = g0 + j
                    ins = e.matmul(ps[bank][:, :], lhsT=dgb[di][:, j, :], rhs=hb[:, c, k:k + TT],
                                   start=(k == 0), stop=(k == CW - 1))
                return ins
            S.add("pe", mfn, reads=[("dg", di), ("h", c)], writes=[("ps", bank)])
        S.add("act", lambda e: e.activation(out=U[:, c, :], in_=ps[bank][:, :], func=AF.Identity,
                                            bias=ppc(l, PP_CB + c)),
              reads=[("ps", bank), ("pp",)], writes=[("U", c)])

    def tile_layer(t, l, full):
        S.add("pool", lambda e: e.tensor_copy(out=hb[:, :, 0:PRE], in_=hst[:, l, :, :]),
              reads=[("hst", l)], writes=[("h", c) for c in range(8)])
        S.add("pool", lambda e: e.tensor_copy(out=kT[:, :, 0:128], in_=kst[:, l, :, :]),
              reads=[("kst", l)], writes=[("k", 0), ("k", 1)])
        S.add("pool", lambda e: e.tensor_copy(out=vaug[:, 0, :, :], in_=vst[:, l, :, :]),
              reads=[("vst", l)], writes=[("v", 0)])

        def proj_fn(bank, slot, j):
            def fn(e):
                ins = None
                for kc in range(KC):
                    ins = e.matmul(ps[bank][:, :], lhsT=wr[slot][:, kc, j * 128:(j + 1) * 128],
                                   rhs=xT[:, kc, :], start=(kc == 0), stop=(kc == KC - 1))
                return ins
            return fn

        iters = [(qb, u, par) for qb in range(NBLK) for u in range(2) for par in range(2)]
        st = {}

        def att_front(i):
            qb, u, par = iters[i]
            pr = slice(par * 64, par * 64 + 64)
            pslot = S.alloc("pT", NPT)
            banks = []
            for kb in range(2):
                bank = psb()
                banks.append(bank)
                k0 = (qb + kb) * 128

                def sfn(e, bank=bank, k0=k0, u=u, pr=pr, qb=qb):
                    return e.matmul(
                        ps[bank][:, :].rearrange("p (a q) -> p a q", a=4),
                        lhsT=kT[pr, u, k0:k0 + 128],
                        rhs=qT[pr, 4 * u:4 * u + 4, qb * 128:(qb + 1) * 128],
                        start=True, stop=True)
                S.add("pe", sfn, reads=[("k", u)] + [("q", 4 * u + a) for a in range(4)],
                      writes=[("ps", bank)])
            for kb in range(2):
                S.add("act", (lambda e, bank=banks[kb], pslot=pslot, kb=kb: e.activation(
                    out=pT[pslot][:, kb, :], in_=ps[bank][:, :], func=AF.Exp, scale=0.125)),
                      reads=[("ps", banks[kb])], writes=[("pT", pslot, kb)])
                if kb == 0:
                    moff = C_MB0 if (t == main_from and qb == 0 and main_from > 0) else C_MB
                else:
                    moff = C_MA

                def mfn(e, pslot=pslot, kb=kb, moff=moff):
                    v_ = pT[pslot][:, kb, :].rearrange("p (a q) -> p a q", a=4)
                    return e.tensor_tensor(
                        out=v_, in0=v_,
                        in1=cs(moff).unsqueeze(1).broadcast_to([128, 4, 128]),
                        op=ALU.mult)
                S.add(mask_eng, mfn, reads=[("pT", pslot, kb), ("cst",)],
                      writes=[("pT", pslot, kb)])
            st[i] = pslot

        def att_back(i):
            qb, u, par = iters[i]
            pslot = st[i]
            orow = slice(par * 64, par * 64 + 64)
            drow = slice((1 - par) * 64, (1 - par) * 64 + 64)
            r = par * 2 + u
            var = u * 2 + par
            bank = psb()

            def pvfn(e):
                e.matmul(ps[bank][:, :], lhsT=vaug[:, qb, var, :], rhs=pT[pslot][:, 0, :],
                         start=True, stop=False)
                e.matmul(ps[bank][:, :], lhsT=vaug[:, qb + 1, var, :], rhs=pT[pslot][:, 1, :],
                         start=False, stop=False)
                return e.matmul(ps[bank][:, :], lhsT=cs(C_OHSK + r * 128, 128, rows=4),
                                rhs=sk4[:, l * 4:(l + 1) * 4, :], start=False, stop=True)
            S.add("pe", pvfn, reads=[("v", qb), ("v", qb + 1), ("pT", pslot, 0),
                                     ("pT", pslot, 1), ("cst",), ("sk4",)],
                  writes=[("ps", bank)])
            t1 = S.alloc("tf", NTF)
            S.add("dve", lambda e: e.reciprocal(out=tf[t1][orow, :], in_=ps[bank][drow, :]),
                  reads=[("ps", bank)], writes=[("tf", t1)])
            t2 = S.alloc("tf", NTF)
            S.add("dve", lambda e: e.tensor_tensor(
                out=tf[t2][orow, :], in0=ps[bank][orow, :], in1=tf[t1][orow, :], op=ALU.mult),
                  reads=[("ps", bank), ("tf", t1)], writes=[("tf", t2)])

            def gfn(e):
                o = gB[orow, 4 * u:4 * u + 4, qb * 128:(qb + 1) * 128]
                return e.tensor_tensor(
                    out=o, in0=o, in1=tf[t2][orow, :].rearrange("p (a q) -> p a q", a=4),
                    op=ALU.mult)
            S.add(gate_eng, gfn, reads=[("tf", t2)] + [("gB", 4 * u + a) for a in range(4)],
                  writes=[("gB", 4 * u + a) for a in range(4)])

        DEPTH_ATT = 2
        att_steps = []
        if full:
            for i in range(min(DEPTH_ATT, len(iters))):
                att_steps.append(lambda i=i: att_front(i))
            for i in range(len(iters)):
                att_steps.append(lambda i=i: att_back(i))
                if i + DEPTH_ATT < len(iters):
                    att_steps.append(lambda i=i: att_front(i + DEPTH_ATT))

        def att_step():
            if att_steps:
                att_steps.pop(0)()

        gorder = [4, 5, 6, 9, 10, 0, 1, 2, 3, 7, 8] if full else [0, 1, 2, 3, 4]
        pending = []
        for gpos, g in enumerate(gorder):
            att_on = full and gpos >= 5
            ready = list(pending)
            del pending[:]
            slot = load_w(l, g)
            for j in range(4):
                if att_on:
                    att_step()
                ci = g * 4 + j
                kind, idx = IN_CHUNKS[ci]
                bcol = ppc(l, ci)
                if kind == "pad":
                    continue
                if kind == "v":
                    bank = psb()

                    def vfn(e, bank=bank, slot=slot, j=j):
                        ins = None
                        for b in range(NBLK):
                            for kc in range(KC):
                                ins = e.matmul(ps[bank][:, b * 128:(b + 1) * 128],
                                               lhsT=xT[:, kc, b * 128:(b + 1) * 128],
                                               rhs=wr[slot][:, kc, j * 128:(j + 1) * 128],
                                               start=(kc == 0), stop=(kc == KC - 1))
                        return ins
                    S.add("pe", vfn, reads=[("w", slot)] + XT_KEYS, writes=[("ps", bank)])
                    for b in range(NBLK):
                        for par in range(2):
                            def vev(e, b=b, par=par, bank=bank):
                                o = vaug[:, 1 + b, par:4:2, par * 64:par * 64 + 64]
                                i0 = ps[bank][:, b * 128:(b + 1) * 128].rearrange("p (u d) -> p u d", u=2)
                                i1 = bvbc[:, l * 128:(l + 1) * 128].rearrange("p (u d) -> p u d", u=2)
                                return e.tensor_tensor(out=o, in0=i0, in1=i1, op=ALU.add)
                            S.add("dve", vev, reads=[("ps", bank), ("bvbc",)], writes=[("v", 1 + b)])
                    continue
                if kind == "glu":
                    continue
                bank = psb()
                S.add("pe", proj_fn(bank, slot, j), reads=[("w", slot)] + XT_KEYS,
                      writes=[("ps", bank)])
                if kind == "val":
                    bank_g = psb()
                    S.add("pe", proj_fn(bank_g, slot, j + 1), reads=[("w", slot)] + XT_KEYS,
                          writes=[("ps", bank_g)])
                    ti = S.alloc("tf", NTF)
                    bg = ppc(l, ci + 1)
                    S.add("act", (lambda e, bank_g=bank_g, ti=ti, bg=bg: e.activation(
                        out=tf[ti][:, :], in_=ps[bank_g][:, :], func=AF.Sigmoid, bias=bg)),
                          reads=[("ps", bank_g), ("pp",)], writes=[("tf", ti)])
                    S.add("dve", (lambda e, bank=bank, ti=ti, idx=idx, bcol=bcol: e.scalar_tensor_tensor(
                        out=hb[:, idx, PRE:PRE + TT], in0=ps[bank][:, :], scalar=bcol,
                        in1=tf[ti][:, :], op0=ALU.add, op1=ALU.mult)),
                          reads=[("ps", bank), ("tf", ti), ("pp",)], writes=[("h", idx)])
                    if full:
                        pending.append(idx)
                    continue
                if kind == "kd":
                    S.add("act", (lambda e, bank=bank, idx=idx, bcol=bcol: e.activation(
                        out=kT[:, idx, 128:640], in_=ps[bank][:, :], func=AF.Identity, bias=bcol)),
                          reads=[("ps", bank), ("pp",)], writes=[("k", idx)])
                elif kind == "q":
                    S.add("act", (lambda e, bank=bank, idx=idx, bcol=bcol: e.activation(
                        out=qT[:, idx, :], in_=ps[bank][:, :], func=AF.Identity, bias=bcol)),
                          reads=[("ps", bank), ("pp",)], writes=[("q", idx)])
                elif kind in ("gA", "gB"):
                    dst = gA if kind == "gA" else gB
                    S.add("act", (lambda e, bank=bank, idx=idx, bcol=bcol, dst=dst: e.activation(
                        out=dst[:, idx, :], in_=ps[bank][:, :], func=AF.Silu, bias=bcol)),
                          reads=[("ps", bank), ("pp",)], writes=[(kind, idx)])

            for c_ in ready:
                conv_pe(l, c_)
                if att_on:
                    att_step()
        for c_ in pending:
            conv_pe(l, c_)
            att_step()

        if t == 0:
            S.add("pool", lambda e: e.tensor_scalar(
                out=hst[:, l, :, :], in0=hb[:, :, TT:TT + PRE], scalar1=mcol[:, 0:1], scalar2=None,
                op0=ALU.mult),
                  reads=[("h", c) for c in range(8)] + [("mcol",)], writes=[("hst", l)])
        else:
            S.add("pool", lambda e: e.tensor_copy(out=hst[:, l, :, :], in_=hb[:, :, TT:TT + PRE]),
                  reads=[("h", c) for c in range(8)], writes=[("hst", l)])
        S.add("pool", lambda e: e.tensor_copy(out=kst[:, l, :, :], in_=kT[:, :, TT:TT + 128]),
              reads=[("k", 0), ("k", 1)], writes=[("kst", l)])
        S.add("pool", lambda e: e.tensor_copy(out=vst[:, l, :, :], in_=vaug[:, 4, :, :]),
              reads=[("v", 4)], writes=[("vst", l)])
        if not full:
            return
        while att_steps:
            att_step()

        b1 = psb()
        b2 = psb()
        for c in range(8):
            i1 = S.alloc("tb", NTB)
            S.add("act", (lambda e, c=c, i1=i1: e.activation(out=tb[i1][:, :], in_=U[:, c, :], func=AF.Copy)),
                  reads=[("U", c)], writes=[("tb", i1)])
            i2 = S.alloc("tb", NTB)
            S.add("act", (lambda e, c=c, i2=i2: e.activation(out=tb[i2][:, :], in_=U[:, c, :], func=AF.Square)),
                  reads=[("U", c)], writes=[("tb", i2)])

            def stf(e, c=c, i1=i1, i2=i2):
                e.matmul(ps[b1][:, :], lhsT=cs(C_ONES), rhs=tb[i1][:, :], start=(c == 0), stop=(c == 7))
                return e.matmul(ps[b2][:, :], lhsT=cs(C_ONES), rhs=tb[i2][:, :], start=(c == 0), stop=(c == 7))
            S.add("pe", stf, reads=[("tb", i1), ("tb", i2), ("cst",)], writes=[("ps", b1), ("ps", b2)])
        imu = S.alloc("tf", NTF)
        S.add("dve", lambda e: e.tensor_scalar(out=tf[imu][:, :], in0=ps[b1][:, :], scalar1=1.0 / CC,
                                               scalar2=None, op0=ALU.mult),
              reads=[("ps", b1)], writes=[("tf", imu)])
        imq = S.alloc("tf", NTF)
        S.add("dve", lambda e: e.tensor_tensor(out=tf[imq][:, :], in0=tf[imu][:, :], in1=tf[imu][:, :],
                                               op=ALU.mult),
              reads=[("tf", imu)], writes=[("tf", imq)])
        ivar = S.alloc("tf", NTF)
        S.add("dve", lambda e: e.scalar_tensor_tensor(out=tf[ivar][:, :], in0=ps[b2][:, :], scalar=1.0 / CC,
                                                      in1=tf[imq][:, :], op0=ALU.mult, op1=ALU.subtract),
              reads=[("ps", b2), ("tf", imq)], writes=[("tf", ivar)])
        S.add("act", lambda e: e.activation(out=tf[imq][:, :], in_=tf[ivar][:, :], func=AF.Sqrt, bias=EPS),
              reads=[("tf", ivar)], writes=[("tf", imq)])
        S.add("dve", lambda e: e.reciprocal(out=tf[ivar][:, :], in_=tf[imq][:, :]),
              reads=[("tf", imq)], writes=[("tf", ivar)])
        irs = ivar
        for c in range(8):
            S.add("dve", (lambda e, c=c: e.tensor_tensor(out=U[:, c, :], in0=U[:, c, :], in1=tf[imu][:, :],
                                                         op=ALU.subtract)),
                  reads=[("U", c), ("tf", imu)], writes=[("U", c)])
            S.add("dve", (lambda e, c=c: e.tensor_tensor(out=U[:, c, :], in0=U[:, c, :], in1=tf[irs][:, :],
                                                         op=ALU.mult)),
                  reads=[("U", c), ("tf", irs)], writes=[("U", c)])
            i1 = S.alloc("tb", NTB)
            S.add("act", (lambda e, c=c, i1=i1: e.activation(
                out=tb[i1][:, :], in_=U[:, c, :], func=AF.Silu, scale=ppc(l, PP_LG + c),
                bias=ppc(l, PP_LB + c))),
                  reads=[("U", c), ("pp",)], writes=[("tb", i1)])
            S.add(gate_eng, (lambda e, c=c, i1=i1: e.tensor_tensor(
                out=gA[:, c, :], in0=gA[:, c, :], in1=tb[i1][:, :], op=ALU.mult)),
                  reads=[("tb", i1), ("gA", c)], writes=[("gA", c)])
        S.add("sp", lambda e: e.dma_start(out=U[:, 0:4, :].rearrange("p a q -> p (a q)"),
                                          in_=lng_in[l:l + 1, :].broadcast_to([128, D])),
              writes=[("U", c) for c in range(4)], dma=("lg", 0))
        S.add("sp", lambda e: e.dma_start(out=U[:, 4:8, :].rearrange("p a q -> p (a q)"),
                                          in_=lnb_in[l:l + 1, :].broadcast_to([128, D])),
              writes=[("U", c) for c in range(4, 8)], dma=("lg", 1))

        for dg in range(4):
            slot = load_w(l, NG_IN + dg)
            for b in range(NBLK):
                bank = psb()

                def ofn(e, bank=bank, slot=slot, b=b, dg=dg):
                    for ec in range(16):
                        ysrc = gA if ec < 8 else gB
                        e.matmul(ps[bank][:, :], lhsT=ysrc[:, ec % 8, b * 128:(b + 1) * 128],
                                 rhs=wr[slot][:, ec, :], start=(ec == 0), stop=False)
                    return e.matmul(ps[bank][:, :], lhsT=cs(C_OHBO + dg * 128, 128, rows=4),
                                    rhs=bo4[:, l * 512:(l + 1) * 512], start=False, stop=True)
                S.add("pe", ofn, reads=[("w", slot), ("cst",), ("bo4",)] +
                      [("gA", c) for c in range(8)] + [("gB", c) for c in range(8)],
                      writes=[("ps", bank)])
                S.add("dve", (lambda e, bank=bank, b=b, dg=dg: e.scalar_tensor_tensor(
                    out=x_tok[:, b, dg * 512:(dg + 1) * 512], in0=x_tok[:, b, dg * 512:(dg + 1) * 512],
                    scalar=ALPHA, in1=ps[bank][:, :], op0=ALU.mult, op1=ALU.add)),
                      reads=[("ps", bank), ("x", b)], writes=[("x", b)])
        last = (l == L - 1)
        for b in range(NBLK):
            for dg in range(4):
                S.add("dve", (lambda e, b=b, dg=dg: e.bn_stats(
                    out=sm[:, b, dg * 6:(dg + 1) * 6], in_=x_tok[:, b, dg * 512:(dg + 1) * 512])),
                      reads=[("x", b)], writes=[("sm", b, dg)])
            S.add("dve", (lambda e, b=b: e.bn_aggr(out=sm[:, b, 24:26], in_=sm[:, b, 0:24])),
                  reads=[("sm", b, dg) for dg in range(4)], writes=[("sm", b, 4)])
            S.add("act", (lambda e, b=b: e.activation(out=sm[:, b, 26:27], in_=sm[:, b, 25:26],
                                                      func=AF.Sqrt, bias=EPS)),
                  reads=[("sm", b, 4)], writes=[("sm", b, 5)])
            S.add("dve", (lambda e, b=b: e.reciprocal(out=sm[:, b, 27:28], in_=sm[:, b, 26:27])),
                  reads=[("sm", b, 5)], writes=[("sm", b, 6)])
            S.add("dve", (lambda e, b=b: e.tensor_scalar(
                out=sm[:, b, 28:29], in0=sm[:, b, 24:25], scalar1=-1.0, scalar2=sm[:, b, 27:28],
                op0=ALU.mult, op1=ALU.mult)),
                  reads=[("sm", b, 4), ("sm", b, 6)], writes=[("sm", b, 7)])
            for hf in range(2):
                cs_ = slice(hf * 1024, (hf + 1) * 1024)
                S.add("dve", (lambda e, b=b, cs_=cs_, hf=hf: e.scalar_tensor_tensor(
                    out=x_tok[:, b, cs_], in0=x_tok[:, b, cs_], scalar=sm[:, b, 24:25],
                    in1=U[:, 2 * hf:2 * hf + 2, :].rearrange("p a q -> p (a q)"),
                    op0=ALU.subtract, op1=ALU.mult)),
                      reads=[("x", b), ("sm", b, 4)] + [("U", c) for c in range(4)], writes=[("x", b)])
                S.add("dve", (lambda e, b=b, cs_=cs_, hf=hf: e.scalar_tensor_tensor(
                    out=x_tok[:, b, cs_], in0=x_tok[:, b, cs_], scalar=sm[:, b, 27:28],
                    in1=U[:, 4 + 2 * hf:6 + 2 * hf, :].rearrange("p a q -> p (a q)"),
                    op0=ALU.mult, op1=ALU.add)),
                      reads=[("x", b), ("sm", b, 6)] + [("U", c) for c in range(4, 8)], writes=[("x", b)])
            if last:
                row = ((t - main_from) * NBLK + b) * 128
                out_ops.append(S.add("sp", (lambda e, b=b, row=row: e.dma_start(
                    out=y_out[row:row + 128, :], in_=x_tok[:, b, :])),
                    reads=[("x", b)], dma=("xo", b)))
            else:
                prep(b)

    out_ops = []
    for t in range(NT):
        for b in range(NBLK):
            row = (t * NBLK + b) * 128
            S.add("sp", (lambda e, b=b, row=row: e.dma_start(out=x_tok[:, b, :], in_=x_in[row:row + 128, :])),
                  writes=[("x", b)], dma=("xi", b))
            prep(b)
        for l in range(L):
            full = not (t < main_from and l == L - 1)
            tile_layer(t, l, full)
    fin = S.add("sp", lambda e: None)
    fin.deps = list(out_ops)

    S.finalize()
    dma_keys = list(S.dma_cnt.keys())
    eng_sems = {}
    for en in S.ENGS:
        n = max(1, -(-S.nticks[en] // EPOCH))
        eng_sems[en] = [es.enter_context(nc.semaphore(f"s_{en}{i}")) for i in range(n)]
    dma_sems = {k: es.enter_context(nc.semaphore("d_" + "_".join(str(x) for x in k))) for k in dma_keys}
    with nc.Block() as block:
        @block.tensor
        def _(e):
            S.emit("pe", e, eng_sems, dma_sems)

        @block.scalar
        def _(e):
            S.emit("act", e, eng_sems, dma_sems)

        @block.vector
        def _(e):
            S.emit("dve", e, eng_sems, dma_sems)

        @block.gpsimd
        def _(e):
            S.emit("pool", e, eng_sems, dma_sems)

        @block.sync
        def _(e):
            S.emit("sp", e, eng_sems, dma_sems)
    es.close()
    return nc


def prep_params(L, w_in, b_in, conv_w, conv_b, conv_ln_g, conv_ln_b, sinks, w_out, b_out,
                ln_g, ln_b):
    f = np.float32
    wall = np.empty((L * NG, 128, 8192), dtype=f)
    pp = np.zeros((128, L * NPP), dtype=f)
    sk = np.zeros((4, L * 4), dtype=f)
    bo = np.zeros((4, L * 512), dtype=f)
    bv = np.zeros((1, L * 128), dtype=f)
    for l in range(L):
        cols = np.concatenate([_chunk_cols(k, i) for (k, i) in IN_CHUNKS])
        wi = np.asarray(w_in[l])[:, cols]
        wi = wi.reshape(KC, 128, NG_IN, 512).transpose(2, 1, 0, 3)
        wall[l * NG:l * NG + NG_IN] = wi.reshape(NG_IN, 128, 8192)
        wo = np.asarray(w_out[l]).reshape(KC, 128, 4, 512).transpose(2, 1, 0, 3)
        wall[l * NG + NG_IN:(l + 1) * NG] = wo.reshape(4, 128, 8192)
        bi = np.asarray(b_in[l])[cols].reshape(44, 128).T
        pp[:, l * NPP:l * NPP + 44] = bi
        cw = np.asarray(conv_w[l]).reshape(CW, 8, 128).transpose(2, 1, 0)
        pp[:, l * NPP + PP_CW:l * NPP + PP_CB] = cw.reshape(128, 8 * CW)
        pp[:, l * NPP + PP_CB:l * NPP + PP_LG] = np.asarray(conv_b[l]).reshape(8, 128).T
        pp[:, l * NPP + PP_LG:l * NPP + PP_LB] = np.asarray(conv_ln_g[l]).reshape(8, 128).T
        pp[:, l * NPP + PP_LB:l * NPP + PP_LB + 8] = np.asarray(conv_ln_b[l]).reshape(8, 128).T
        sl = np.asarray(sinks[l])
        for par in range(2):
            for u in range(2):
                for k in range(4):
                    sk[par * 2 + u, l * 4 + k] = sl[8 * u + 2 * k + par]
        bo[:, l * 512:(l + 1) * 512] = np.asarray(b_out[l]).reshape(4, 512)
        bv[0, l * 128:(l + 1) * 128] = np.asarray(b_in[l])[4224:4352]
    return dict(wall=wall, pp_in=pp, sk_in=sk, bo_in=bo, bv_in=bv,
                lng_in=np.ascontiguousarray(np.asarray(ln_g)[:L], dtype=f),
                lnb_in=np.ascontiguousarray(np.asarray(ln_b)[:L], dtype=f))


def prep_consts(m):
    f = np.float32
    cst = np.zeros((128, NCST), dtype=f)
    s = np.arange(128)[:, None]
    q = np.arange(128)[None, :]
    cst[:, C_MA:C_MA + 128] = (q >= s)
    cst[:, C_MB:C_MB + 128] = (q < s)
    cst[:, C_MB0:C_MB0 + 128] = (q < s) * float(m)
    cst[:, C_ID:C_ID + 128] = np.eye(128)
    cst[:, C_ONES:C_ONES + 128] = 1.0
    for dg in range(4):
        cst[dg, C_OHBO + dg * 128:C_OHBO + (dg + 1) * 128] = 1.0
    for par in range(2):
        for u in range(2):
            r = par * 2 + u
            lo = (1 - par) * 64
            cst[r, C_OHSK + r * 128 + lo:C_OHSK + r * 128 + lo + 64] = 1.0
    mcol = np.full((128, 1), float(m), dtype=f)
    return cst, mcol


_NC_CACHE = {}


def kernel(x, w_in, b_in, conv_w, conv_b, conv_ln_g, conv_ln_b, sinks, w_out, b_out, ln_g, ln_b):
    L, NT = 4, 5
    x = np.asarray(x, dtype=np.float32)
    B, SEQ, _ = x.shape
    params = prep_params(L, w_in, b_in, conv_w, conv_b, conv_ln_g, conv_ln_b, sinks, w_out,
                         b_out, ln_g, ln_b)
    key = (L, NT)
    if key not in _NC_CACHE:
        _NC_CACHE[key] = build_nc(L, NT)
    nc = _NC_CACHE[key]
    in_maps = []
    for c in range(8):
        b, half = c // 2, c % 2
        t0 = half * 2048
        xin = np.zeros((NT * TT, D), dtype=np.float32)
        if half == 0:
            xin[TT:] = x[b, 0:2048]
        else:
            xin[:] = x[b, t0 - TT:t0 + 2048]
        cst, mcol = prep_consts(half)
        mp = dict(params)
        mp.update(x_in=xin, cst_in=cst, m_in=mcol)
        in_maps.append(mp)
    res = run_bass_kernel_spmd(nc, in_maps, core_ids=list(range(8)))
    out = np.empty((B, SEQ, D), dtype=np.float32)
    for c in range(8):
        b, half = c // 2, c % 2
        out[b, half * 2048:(half + 1) * 2048] = res.results[c]["y"]
    return out
```
